# Optimizing a Trainium2 kernel written in Bass

```python
import math
import jax, jax.numpy as jnp
from jax import lax
import numpy as np

D_MODEL = 1024
BATCH = 8
SEQ = 4096
DEPTH = 4

GRID_W = 64
CTX_LEN = 256

A_HEADS = 8
A_KV_HEADS = 2
A_HEAD_DIM = 64
Q_BLOCK = 128
ROPE_THETA = 10000.0
B_HEADS = 8
B_HEAD_DIM = 64
NA_ROWS = 8
NA_COLS = 16
C_HEADS = 8
C_HEAD_DIM = 64
CONV_K = 4
CHUNK = 64
N_BRANCH = 3
BRANCH_W = 512
N_EXPERTS = 16
D_EXPERT = 2048
CAPACITY = 2
N_MOD = 6
LN_EPS = 1e-6
DEEPNORM_ALPHA = (2 * DEPTH) ** 0.25
DEEPNORM_BETA = (8 * DEPTH) ** -0.25

A_Q_W = A_HEADS * A_HEAD_DIM
A_KV_W = A_KV_HEADS * A_HEAD_DIM
B_W = B_HEADS * B_HEAD_DIM
C_W = C_HEADS * C_HEAD_DIM
IN_SIZES = (A_Q_W, A_KV_W, A_KV_W, B_W, B_W, B_W, 3 * C_W, C_W, 2 * C_HEADS, 2 * C_HEADS, N_BRANCH * D_MODEL)
D_IN = sum(IN_SIZES)

kernel_name = 'hybrid_diffusion_trunk'


def layer_norm(x):
    xf = x.astype(jnp.float32)
    mu = jnp.mean(xf, axis=-1, keepdims=True)
    var = jnp.mean(jnp.square(xf - mu), axis=-1, keepdims=True)
    return ((xf - mu) * lax.rsqrt(var + LN_EPS)).astype(x.dtype)


def layer_norm_affine(x, g, b):
    return layer_norm(x) * g + b


def rms_norm(x, g):
    xf = x.astype(jnp.float32)
    y = xf * lax.rsqrt(jnp.mean(jnp.square(xf), axis=-1, keepdims=True) + LN_EPS)
    return y.astype(x.dtype) * g


def l2_normalize(x):
    return x * lax.rsqrt(jnp.sum(jnp.square(x), axis=-1, keepdims=True) + LN_EPS)


def modulate(x, shift, scale):
    return x * (1 + scale) + shift


def softmax_f32(s):
    return jax.nn.softmax(s.astype(jnp.float32), axis=-1)


def split_in(z):
    return jnp.split(z, np.cumsum(IN_SIZES)[:-1].tolist(), axis=-1)


def heads(t, n_heads, head_dim):
    return t.reshape(t.shape[0], t.shape[1], n_heads, head_dim)


def rope_tables(n_tokens, dtype):
    t = jnp.arange(n_tokens)
    rows = (t // GRID_W).astype(jnp.float32)
    cols = (t % GRID_W).astype(jnp.float32)
    n_freq = A_HEAD_DIM // 4
    inv_freq = ROPE_THETA ** (-jnp.arange(n_freq, dtype=jnp.float32) / n_freq)
    ang_r = rows[:, None] * inv_freq
    ang_c = cols[:, None] * inv_freq
    return tuple(a.astype(dtype) for a in (jnp.cos(ang_r), jnp.sin(ang_r), jnp.cos(ang_c), jnp.sin(ang_c)))


def apply_rope_2d(x, tables):
    cr, sr, cc, sc = [t[None, :, None, :] for t in tables]
    x1, x2, x3, x4 = jnp.split(x, 4, axis=-1)
    return jnp.concatenate([x1 * cr - x2 * sr, x2 * cr + x1 * sr,
                            x3 * cc - x4 * sc, x4 * cc + x3 * sc], axis=-1)


def dense_attention(q, k, v):
    b, m, h, d = q.shape
    kvh = k.shape[2]
    qg = q.reshape(b, m, kvh, h // kvh, d)
    s = jnp.einsum('bqkgd,bmkd->bkgqm', qg, k) * (d ** -0.5)
    p = softmax_f32(s).astype(v.dtype)
    return jnp.einsum('bkgqm,bmkd->bqkgd', p, v).reshape(b, m, h * d)


def gqa_latent(q, k_all, v_all):
    b, n = q.shape[:2]
    g = A_HEADS // A_KV_HEADS
    nb = n // Q_BLOCK
    qb = q.reshape(b, nb, Q_BLOCK, A_KV_HEADS, g, A_HEAD_DIM).transpose(1, 0, 3, 4, 2, 5)
    scale = A_HEAD_DIM ** -0.5

    def block(qi):
        s = jnp.einsum('bkgqd,bkmd->bkgqm', qi, k_all) * scale
        p = softmax_f32(s).astype(v_all.dtype)
        return jnp.einsum('bkgqm,bkmd->bkgqd', p, v_all)

    o = lax.map(block, qb)
    return o.transpose(1, 0, 4, 2, 3, 5).reshape(b, n, A_HEADS * A_HEAD_DIM)


def neighborhood_latent(q, k, v, k_ctx, v_ctx, rpb):
    b, n = q.shape[:2]
    rows = n // GRID_W
    win_r = min(NA_ROWS, rows)

    def to_grid(t):
        return t.reshape(b, rows, GRID_W, B_HEADS, B_HEAD_DIM).transpose(0, 3, 1, 2, 4)

    qg, kg, vg = to_grid(q), to_grid(k), to_grid(v)
    kc = k_ctx.transpose(0, 2, 1, 3)
    vc = v_ctx.transpose(0, 2, 1, 3)
    cols = np.arange(GRID_W)
    col_start = np.clip(cols - NA_COLS // 2, 0, GRID_W - NA_COLS)
    col_idx = col_start[:, None] + np.arange(NA_COLS)[None, :]
    col_off = col_idx - cols[:, None] + (NA_COLS - 1)
    rpb_cols = rpb[:, :, col_off]
    scale = B_HEAD_DIM ** -0.5

    def row_block(r):
        rs = jnp.clip(r - win_r // 2, 0, rows - win_r)
        qr = lax.dynamic_index_in_dim(qg, r, axis=2, keepdims=False)
        k_rows = lax.dynamic_slice_in_dim(kg, rs, win_r, axis=2)
        v_rows = lax.dynamic_slice_in_dim(vg, rs, win_r, axis=2)
        k_nb = k_rows[:, :, :, col_idx]
        v_nb = v_rows[:, :, :, col_idx]
        row_off = rs + jnp.arange(win_r) - r + (NA_ROWS - 1)
        bias = jnp.take(rpb_cols, row_off, axis=1).transpose(0, 2, 1, 3)
        s_nb = jnp.einsum('bhqd,bhrqjd->bhqrj', qr, k_nb) * scale + bias[None]
        s_nb = s_nb.reshape(b, B_HEADS, GRID_W, win_r * NA_COLS)
        s_ctx = jnp.einsum('bhqd,bhmd->bhqm', qr, kc) * scale
        p = softmax_f32(jnp.concatenate([s_nb, s_ctx], axis=-1)).astype(v.dtype)
        p_nb = p[..., :win_r * NA_COLS].reshape(b, B_HEADS, GRID_W, win_r, NA_COLS)
        p_ctx = p[..., win_r * NA_COLS:]
        return (jnp.einsum('bhqrj,bhrqjd->bhqd', p_nb, v_nb)
                + jnp.einsum('bhqm,bhmd->bhqd', p_ctx, vc))

    o = lax.map(row_block, jnp.arange(rows))
    return o.transpose(1, 0, 3, 2, 4).reshape(b, n, B_HEADS * B_HEAD_DIM)


def short_conv(x, w):
    n = x.shape[1]
    left = CONV_K // 2
    xp = jnp.pad(x, ((0, 0), (left, CONV_K - 1 - left), (0, 0)))
    y = xp[:, 0:n] * w[0]
    for j in range(1, CONV_K):
        y = y + xp[:, j:j + n] * w[j]
    return jax.nn.silu(y)


def gdn_prepare(qkv, a, bt, a_log, dt_bias):
    b, n = qkv.shape[:2]
    q, k, v = jnp.split(qkv.astype(jnp.float32), 3, axis=-1)
    q = l2_normalize(heads(q, C_HEADS, C_HEAD_DIM)) * (C_HEAD_DIM ** -0.5)
    k = l2_normalize(heads(k, C_HEADS, C_HEAD_DIM))
    v = heads(v, C_HEADS, C_HEAD_DIM)
    a = a.astype(jnp.float32).reshape(b, n, 2, C_HEADS)
    bt = bt.astype(jnp.float32).reshape(b, n, 2, C_HEADS)
    g = -jnp.exp(a_log.astype(jnp.float32)) * jax.nn.softplus(a + dt_bias.astype(jnp.float32))
    beta = jax.nn.sigmoid(bt)

    def both(t):
        return jnp.stack([t, jnp.flip(t, axis=1)], axis=0).transpose(0, 1, 3, 2, 4)

    def both_dir(t):
        return jnp.stack([t[:, :, 0], jnp.flip(t[:, :, 1], axis=1)], axis=0).transpose(0, 1, 3, 2)

    return both(q), both(k), both(v), both_dir(g), both_dir(beta)


def gated_delta_rule(q, k, v, g, beta, s0, with_output):
    lead = q.shape[:-2]
    nd = len(lead)
    n = q.shape[-2]
    nc = n // CHUNK

    def chunks(t):
        return t.reshape(lead + (nc, CHUNK) + t.shape[nd + 1:])

    qc, kc, vc, gc, bc = chunks(q), chunks(k), chunks(v), chunks(g), chunks(beta)
    gcum = jnp.cumsum(gc, axis=-1)
    idx = jnp.arange(CHUNK)
    incl = idx[:, None] >= idx[None, :]
    strict = idx[:, None] > idx[None, :]
    decay = jnp.exp(jnp.where(incl, gcum[..., :, None] - gcum[..., None, :], -jnp.inf))
    kb = kc * bc[..., None]
    vb = vc * bc[..., None]
    lmat = jnp.where(strict, jnp.einsum('...id,...jd->...ij', kb, kc) * decay, 0.0)
    eye = jnp.eye(CHUNK, dtype=jnp.float32)
    tmat = lax.linalg.triangular_solve(eye + lmat, jnp.broadcast_to(eye, lmat.shape),
                                       left_side=True, lower=True, unit_diagonal=True)
    u = tmat @ vb
    w = tmat @ (kb * jnp.exp(gcum)[..., None])
    glast = gcum[..., -1:]
    kdec = kc * jnp.exp(glast - gcum)[..., None]
    sdec = jnp.exp(glast[..., 0])
    xs = (u, w, kdec, sdec)
    if with_output:
        qdec = qc * jnp.exp(gcum)[..., None]
        intra = jnp.where(incl, jnp.einsum('...id,...jd->...ij', qc, kc) * decay, 0.0)
        xs = xs + (qdec, intra)
    xs = tuple(jnp.moveaxis(t, nd, 0) for t in xs)

    def step(s, inp):
        u_i, w_i, kd_i, sd_i = inp[:4]
        v_new = u_i - w_i @ s
        s_next = s * sd_i[..., None, None] + jnp.swapaxes(kd_i, -1, -2) @ v_new
        if with_output:
            qd_i, intra_i = inp[4:]
            return s_next, qd_i @ s + intra_i @ v_new
        return s_next, None

    s_fin, o = lax.scan(step, s0, xs)
    if not with_output:
        return None, s_fin
    o = jnp.moveaxis(o, 0, nd).reshape(lead + (n, v.shape[-1]))
    return o, s_fin


def gdn_branch(qkv, a, bt, gate, a_log, dt_bias, o_gain, s0, with_output):
    b, n = qkv.shape[:2]
    q2, k2, v2, g2, beta2 = gdn_prepare(qkv, a, bt, a_log, dt_bias)
    o2, s_fin = gated_delta_rule(q2, k2, v2, g2, beta2, s0, with_output)
    if not with_output:
        return None, s_fin
    o = (o2[0] + jnp.flip(o2[1], axis=-2)).transpose(0, 2, 1, 3)
    o = rms_norm(o, o_gain) * jax.nn.silu(heads(gate.astype(jnp.float32), C_HEADS, C_HEAD_DIM))
    return o.reshape(b, n, C_W).astype(qkv.dtype), s_fin


def merge_branches(branches, gate_logits, w_branch, w_out):
    b, n = gate_logits.shape[:2]
    gates = jax.nn.sigmoid(gate_logits.astype(jnp.float32)).astype(gate_logits.dtype)
    gates = gates.reshape(b, n, N_BRANCH, D_MODEL)
    m = gates[:, :, 0] * (branches[0] @ w_branch[0])
    for i in range(1, N_BRANCH):
        m = m + gates[:, :, i] * (branches[i] @ w_branch[i])
    return m @ w_out


def token_mixers(h, h_x, rope, w_in, qk_gain, rpb, conv_w, a_log, dt_bias, o_gain, w_branch, w_out, need_ctx):
    b = h.shape[0]
    qa, ka, va, qb, kb, vb, qkv_c, gate_c, a_c, b_c, mg = split_in(h @ w_in)
    qa_x, ka_x, va_x, qb_x, kb_x, vb_x, qkv_cx, gate_cx, a_cx, b_cx, mg_x = split_in(h_x @ w_in)
    qa = apply_rope_2d(rms_norm(heads(qa, A_HEADS, A_HEAD_DIM), qk_gain[0]), rope)
    ka = apply_rope_2d(rms_norm(heads(ka, A_KV_HEADS, A_HEAD_DIM), qk_gain[1]), rope)
    ka_x = rms_norm(heads(ka_x, A_KV_HEADS, A_HEAD_DIM), qk_gain[1])
    va_x = heads(va_x, A_KV_HEADS, A_HEAD_DIM)
    k_all = jnp.concatenate([ka, ka_x], axis=1).transpose(0, 2, 1, 3)
    v_all = jnp.concatenate([heads(va, A_KV_HEADS, A_HEAD_DIM), va_x], axis=1).transpose(0, 2, 1, 3)
    o_a = gqa_latent(qa, k_all, v_all)
    kb_x = heads(kb_x, B_HEADS, B_HEAD_DIM)
    vb_x = heads(vb_x, B_HEADS, B_HEAD_DIM)
    o_b = neighborhood_latent(heads(qb, B_HEADS, B_HEAD_DIM), heads(kb, B_HEADS, B_HEAD_DIM),
                              heads(vb, B_HEADS, B_HEAD_DIM), kb_x, vb_x, rpb)
    s0 = jnp.zeros((2, b, C_HEADS, C_HEAD_DIM, C_HEAD_DIM), jnp.float32)
    o_c_x, s_ctx = gdn_branch(short_conv(qkv_cx, conv_w), a_cx, b_cx, gate_cx, a_log, dt_bias,
                              o_gain, s0, need_ctx)
    o_c, _ = gdn_branch(short_conv(qkv_c, conv_w), a_c, b_c, gate_c, a_log, dt_bias,
                        o_gain, s_ctx, True)
    out = merge_branches((o_a, o_b, o_c), mg, w_branch, w_out)
    if not need_ctx:
        return out, None
    o_a_x = dense_attention(rms_norm(heads(qa_x, A_HEADS, A_HEAD_DIM), qk_gain[0]), ka_x, va_x)
    o_b_x = dense_attention(heads(qb_x, B_HEADS, B_HEAD_DIM), kb_x, vb_x)
    out_x = merge_branches((o_a_x, o_b_x, o_c_x), mg_x, w_branch, w_out)
    return out, out_x


def expert_choice_ffn(h, w_router, w_gate_up, w_down):
    b, n, _ = h.shape
    cap = CAPACITY * n // N_EXPERTS
    aff = jax.nn.softmax(jnp.einsum('bnd,de->bne', h, w_router).astype(jnp.float32), axis=-1)
    gate, idx = lax.top_k(jnp.swapaxes(aff, 1, 2), cap)
    bidx = jnp.arange(b)[:, None, None]
    xs = h[bidx, idx]
    gu = jnp.einsum('becd,edf->becf', xs, w_gate_up)
    g_, u_ = jnp.split(gu, 2, axis=-1)
    y = jnp.einsum('becf,efd->becd', jax.nn.silu(g_) * u_, w_down)
    y = y * gate[..., None].astype(y.dtype)
    return jnp.zeros_like(h).at[bidx, idx].add(y)


def setup_inputs(seed: int = 0) -> dict:
    key = jax.random.key(seed)
    ks = jax.random.split(key, 24)
    f32 = jnp.float32

    def nrm(k, shape, s):
        return jax.random.normal(k, shape, f32) * s

    x = nrm(ks[0], (BATCH, SEQ, D_MODEL), 1.0)
    c = nrm(ks[1], (BATCH, D_MODEL), 1.0)
    ctx = nrm(ks[2], (BATCH, CTX_LEN, D_MODEL), 1.0)
    c_ctx = nrm(ks[3], (D_MODEL,), 1.0)
    w_mod = nrm(ks[4], (DEPTH, D_MODEL, N_MOD * D_MODEL), 0.5 * D_MODEL ** -0.5)
    b_mod = nrm(ks[5], (DEPTH, N_MOD * D_MODEL), 0.02)
    w_in = nrm(ks[6], (DEPTH, D_MODEL, D_IN), D_MODEL ** -0.5)
    qk_gain = 1.0 + nrm(ks[7], (DEPTH, 2, A_HEAD_DIM), 0.05)
    rpb = nrm(ks[8], (DEPTH, B_HEADS, 2 * NA_ROWS - 1, 2 * NA_COLS - 1), 0.1)
    conv_w = nrm(ks[9], (DEPTH, CONV_K, 3 * C_W), CONV_K ** -0.5)
    a_log = jnp.log(jax.random.uniform(ks[10], (DEPTH, 2, C_HEADS), f32, 1.0, 16.0))
    dt = jnp.exp(jax.random.uniform(ks[11], (DEPTH, 2, C_HEADS), f32, math.log(1e-3), math.log(1e-1)))
    dt_bias = dt + jnp.log(-jnp.expm1(-dt))
    o_gain = 1.0 + nrm(ks[12], (DEPTH, C_HEAD_DIM), 0.05)
    w_branch = nrm(ks[13], (DEPTH, N_BRANCH, BRANCH_W, D_MODEL), BRANCH_W ** -0.5)
    w_out = nrm(ks[14], (DEPTH, D_MODEL, D_MODEL), D_MODEL ** -0.5 * DEEPNORM_BETA)
    ln1_g = 1.0 + nrm(ks[15], (DEPTH, D_MODEL), 0.05)
    ln1_b = nrm(ks[16], (DEPTH, D_MODEL), 0.02)
    w_router = nrm(ks[17], (DEPTH, D_MODEL, N_EXPERTS), D_MODEL ** -0.5)
    w_gate_up = nrm(ks[18], (DEPTH, N_EXPERTS, D_MODEL, 2 * D_EXPERT), D_MODEL ** -0.5)
    w_down = nrm(ks[19], (DEPTH, N_EXPERTS, D_EXPERT, D_MODEL), D_EXPERT ** -0.5 * DEEPNORM_BETA)
    ln2_g = 1.0 + nrm(ks[20], (DEPTH, D_MODEL), 0.05)
    ln2_b = nrm(ks[21], (DEPTH, D_MODEL), 0.02)
    return {'x': x, 'c': c, 'ctx': ctx, 'c_ctx': c_ctx, 'w_mod': w_mod, 'b_mod': b_mod,
            'w_in': w_in, 'qk_gain': qk_gain, 'rpb': rpb, 'conv_w': conv_w, 'a_log': a_log,
            'dt_bias': dt_bias, 'o_gain': o_gain, 'w_branch': w_branch, 'w_out': w_out,
            'ln1_g': ln1_g, 'ln1_b': ln1_b, 'w_router': w_router, 'w_gate_up': w_gate_up,
            'w_down': w_down, 'ln2_g': ln2_g, 'ln2_b': ln2_b}


def reference(x, c, ctx, c_ctx, w_mod, b_mod, w_in, qk_gain, rpb, conv_w, a_log, dt_bias, o_gain,
              w_branch, w_out, ln1_g, ln1_b, w_router, w_gate_up, w_down, ln2_g, ln2_b):
    rope = rope_tables(x.shape[1], x.dtype)
    c_act = jax.nn.silu(c)
    cc_act = jax.nn.silu(c_ctx)
    for layer in range(DEPTH):
        need_ctx = layer < DEPTH - 1
        mod = jnp.split((c_act @ w_mod[layer] + b_mod[layer])[:, None, :], N_MOD, axis=-1)
        mod_x = jnp.split(cc_act @ w_mod[layer] + b_mod[layer], N_MOD, axis=-1)
        h = modulate(layer_norm(x), mod[0], mod[1])
        h_x = modulate(layer_norm(ctx), mod_x[0], mod_x[1])
        mix, mix_x = token_mixers(h, h_x, rope, w_in[layer], qk_gain[layer], rpb[layer], conv_w[layer],
                                  a_log[layer], dt_bias[layer], o_gain[layer], w_branch[layer],
                                  w_out[layer], need_ctx)
        x = layer_norm_affine(DEEPNORM_ALPHA * x + mod[2] * mix, ln1_g[layer], ln1_b[layer])
        h = modulate(layer_norm(x), mod[3], mod[4])
        moe = expert_choice_ffn(h, w_router[layer], w_gate_up[layer], w_down[layer])
        x = layer_norm_affine(DEEPNORM_ALPHA * x + mod[5] * moe, ln2_g[layer], ln2_b[layer])
        if need_ctx:
            ctx = layer_norm_affine(DEEPNORM_ALPHA * ctx + mod_x[2] * mix_x, ln1_g[layer], ln1_b[layer])
            h_x = modulate(layer_norm(ctx), mod_x[3], mod_x[4])
            moe_x = expert_choice_ffn(h_x, w_router[layer], w_gate_up[layer], w_down[layer])
            ctx = layer_norm_affine(DEEPNORM_ALPHA * ctx + mod_x[5] * moe_x, ln2_g[layer], ln2_b[layer])
    return x
```

```python
import numpy as np
import concourse.bass as bass
import concourse.mybir as mybir
from concourse.bass_utils import run_bass_kernel_spmd
from contextlib import ExitStack

F32 = mybir.dt.float32
BF16 = mybir.dt.bfloat16
I32 = mybir.dt.int32
AF = mybir.ActivationFunctionType
ALU = mybir.AluOpType
AX = mybir.AxisListType


class Buf:
    __slots__ = ("name", "w", "rs")

    def __init__(self, name):
        self.name = name
        self.w = None
        self.rs = []


class Op:
    __slots__ = ("eng", "fn", "deps", "signal", "sem", "val", "dma", "idx", "cc")


class V:
    __slots__ = ("ap", "bs")

    def __init__(self, ap, bs):
        self.ap = ap
        self.bs = bs

    def __getitem__(self, k):
        return V(self.ap[k], self.bs)

    def bitcast(self, dt):
        return V(self.ap.bitcast(dt), self.bs)

    def rearrange(self, s, **kw):
        return V(self.ap.rearrange(s, **kw), self.bs)

    def bc(self, shape):
        return V(self.ap.to_broadcast(shape), self.bs)

    def pbc(self, n):
        return V(self.ap.partition_broadcast(n), self.bs)


class T:
    def __init__(self, ap, name, nbuf=1):
        self.ap = ap
        self.bufs = [Buf(f"{name}.{i}") for i in range(nbuf)]

    def __getitem__(self, k):
        return V(self.ap[k], self.bufs)

    def all(self):
        return V(self.ap, self.bufs)

    def at(self, i, k=None):
        if len(self.bufs) == 1:
            bs = self.bufs
        else:
            bs = [self.bufs[j] for j in i] if isinstance(i, (list, tuple, range)) else [self.bufs[i]]
        return V(self.ap if k is None else self.ap[k], bs)


ENGS = ("pe", "act", "dve", "pool", "sp")
SBUF_LIMIT = 52800 * 4


class Prog:
    def __init__(self, nc):
        self.nc = nc
        self.ops = {e: [] for e in ENGS}
        self.sb_top = 0
        self.sb_peak = 0
        self.n_alloc = 0
        self.dma_pool_n = {"sp": 14, "pool": 14, "act": 8}
        self.dma_cnt = {q: 0 for q in self.dma_pool_n}
        self.dma_last = {q: [None] * n for q, n in self.dma_pool_n.items()}
        self.dma_since_barrier = []
        self.arena = nc.alloc_sbuf_tensor("arena", [128, SBUF_LIMIT // 4], F32).ap()
        self.psum = []
        for i in range(8):
            h = nc.alloc_psum_tensor(f"psum{i}", [128, 512], F32)
            self.psum.append(T(h.ap() if hasattr(h, "ap") else h, f"psum{i}"))

    def sb(self, name, shape, dtype, nbuf=1):
        esz = {F32: 4, BF16: 2, I32: 4}.get(dtype, 4)
        free = 1
        for s in shape[1:]:
            free *= s
        nbytes = (free * esz + 63) // 64 * 64
        off = self.sb_top
        assert off + nbytes <= SBUF_LIMIT, f"SBUF overflow allocating {name}: {off}+{nbytes}"
        self.n_alloc += 1
        ap = self.arena[:shape[0], off // 4:(off + nbytes) // 4]
        if dtype != F32:
            ap = ap.bitcast(dtype)
        ap = ap[:, 0:free]
        if len(shape) == 3:
            ap = ap.rearrange("p (a b) -> p a b", b=shape[2])
        elif len(shape) == 4:
            ap = ap.rearrange("p (a b c) -> p a b c", b=shape[2], c=shape[3])
        self.sb_top = off + nbytes
        self.sb_peak = max(self.sb_peak, self.sb_top)
        return T(ap, name, nbuf)

    def mark(self):
        return self.sb_top

    def release(self, m):
        self.barrier()
        self.sb_top = m

    def dram(self, name, shape, dtype, nbuf=1, kind="Internal"):
        h = self.nc.dram_tensor(name, list(shape), dtype, kind=kind)
        return T(h.ap(), name, nbuf)

    def add(self, eng, fn, reads=(), writes=(), dma=False, cc=False):
        op = Op()
        op.cc = cc
        op.eng = eng
        op.fn = fn
        op.dma = dma
        op.signal = dma
        op.sem = None
        op.val = 0
        deps = {}

        def dep(d, kind):
            if d is None or d is op:
                return
            if (not dma) and (not d.dma) and d.eng == eng:
                if eng == "pe" or kind != "raw":
                    return
            deps[id(d)] = d

        for b in reads:
            dep(b.w, "raw")
        for b in writes:
            dep(b.w, "waw")
            for r in b.rs:
                dep(r, "war")
        if cc:
            self.dma_since_barrier.append(op)
            self.n_cc = getattr(self, "n_cc", 0) + 1
            op.idx = self.n_cc
        elif dma:
            q = eng
            n = self.dma_pool_n[q]
            slot = self.dma_cnt[q] % n
            prev = self.dma_last[q][slot]
            if prev is not None:
                deps[id(prev)] = prev
            op.idx = self.dma_cnt[q]
            self.dma_last[q][slot] = op
            self.dma_cnt[q] += 1
            self.dma_since_barrier.append(op)
        op.deps = list(deps.values())
        for d in op.deps:
            d.signal = True
        for b in reads:
            if not dma:
                b.rs = [r for r in b.rs if r.dma or r.eng != eng]
            b.rs.append(op)
        for b in writes:
            b.w = op
            b.rs = []
        self.ops[eng].append(op)
        return op

    def barrier(self):
        lasts = []
        for e in ENGS:
            for o in reversed(self.ops[e]):
                if o.fn is not None and not o.dma:
                    lasts.append(o)
                    break
        dmas = list(self.dma_since_barrier)
        self.dma_since_barrier = []
        for e in ENGS:
            op = Op()
            op.cc = False
            op.eng = e
            op.fn = None
            op.dma = False
            op.signal = False
            op.sem = None
            op.val = 0
            op.deps = [o for o in lasts if o.eng != e and not o.dma] + dmas
            for d in op.deps:
                d.signal = True
            self.ops[e].append(op)

    @staticmethod
    def _bs(*vs):
        out = []
        for v in vs:
            if isinstance(v, V):
                out.extend(v.bs)
        return out

    @staticmethod
    def _ap(v):
        return v.ap if isinstance(v, V) else v

    def matmul(self, out, lhsT, rhs, start=True, stop=True):
        return self.add("pe", lambda e: e.matmul(out.ap, lhsT.ap, rhs.ap, start=start, stop=stop),
                        reads=self._bs(lhsT, rhs), writes=out.bs)

    def transpose(self, out, in_, ident):
        return self.add("pe", lambda e: e.transpose(out.ap, in_.ap, ident.ap),
                        reads=self._bs(in_, ident), writes=out.bs)

    def act(self, out, in_, func, bias=None, scale=1.0, accum=None, eng="act"):
        kw = {}
        if bias is not None:
            kw["bias"] = self._ap(bias)
        if accum is not None:
            kw["accum_out"] = accum.ap
        sc = self._ap(scale)
        return self.add(eng, lambda e: e.activation(out.ap, in_.ap, func, scale=sc, **kw),
                        reads=self._bs(in_, bias, scale), writes=self._bs(out, accum))

    def tt(self, eng, out, a, b, op):
        return self.add(eng, lambda e: e.tensor_tensor(out.ap, a.ap, b.ap, op),
                        reads=self._bs(a, b), writes=out.bs)

    def ts(self, eng, out, a, s1, op0, s2=None, op1=None, accum=None):
        s1a, s2a = self._ap(s1), self._ap(s2)
        kw = {}
        if op1 is not None:
            kw["op1"] = op1
        if accum is not None:
            kw["accum_out"] = accum.ap
        return self.add(eng, lambda e: e.tensor_scalar(out.ap, a.ap, s1a, s2a, op0, **kw),
                        reads=self._bs(a, s1, s2), writes=self._bs(out, accum))

    def stt(self, eng, out, a, s, b, op0, op1):
        sa = self._ap(s)
        return self.add(eng, lambda e: e.scalar_tensor_tensor(out.ap, a.ap, sa, b.ap, op0, op1),
                        reads=self._bs(a, s, b), writes=out.bs)

    def copy(self, eng, out, in_):
        if eng == "act":
            return self.add(eng, lambda e: e.copy(out.ap, in_.ap), reads=in_.bs, writes=out.bs)
        return self.add(eng, lambda e: e.tensor_copy(out.ap, in_.ap), reads=in_.bs, writes=out.bs)

    def memset(self, eng, out, val):
        return self.add(eng, lambda e: e.memset(out.ap, val), reads=(), writes=out.bs)

    def reduce(self, eng, out, in_, op, axis=AX.X):
        return self.add(eng, lambda e: e.tensor_reduce(out.ap, in_.ap, axis, op), reads=in_.bs, writes=out.bs)

    def recip(self, out, in_):
        return self.add("dve", lambda e: e.reciprocal(out.ap, in_.ap), reads=in_.bs, writes=out.bs)

    def dma(self, q, out, in_, **kw):
        return self.add(q, lambda e: e.dma_start(out.ap, in_.ap, **kw), reads=in_.bs, writes=out.bs, dma=True)

    def allgather(self, out, in_):
        rg = [list(range(8))]
        return self.add("pool", lambda e: e.collective_compute("AllGather", ALU.bypass, replica_groups=rg,
                                                               ins=[in_.ap.opt()], outs=[out.ap.opt()]),
                        reads=in_.bs, writes=out.bs, dma=True, cc=True)

    def generic(self, eng, fn, reads, writes):
        return self.add(eng, fn, reads=self._bs(*reads), writes=self._bs(*writes))

    def emit(self, final_dmas=None):
        nc = self.nc
        self.barrier()
        ROLL = 12000
        with ExitStack() as st:
            for e in ENGS:
                cnt = 0
                sem = None
                k = 0
                for op in self.ops[e]:
                    if op.dma or not op.signal:
                        continue
                    if sem is None or cnt >= ROLL:
                        sem = st.enter_context(nc.semaphore(f"s_{e}_{k}"))
                        k += 1
                        cnt = 0
                    cnt += 1
                    op.sem = sem
                    op.val = cnt
            for q, n in self.dma_pool_n.items():
                sems = [st.enter_context(nc.semaphore(f"d_{q}_{i}")) for i in range(n)] if self.dma_cnt[q] else []
                for op in self.ops[q]:
                    if op.dma and not op.cc:
                        op.sem = sems[op.idx % n]
                        op.val = 16 * (op.idx // n + 1)
            if getattr(self, "n_cc", 0):
                ccsem = st.enter_context(nc.semaphore("cc_sem"))
                for op in self.ops["pool"]:
                    if op.cc:
                        op.sem = ccsem
                        op.val = op.idx
            block = st.enter_context(nc.Block())
            engmap = {"pe": block.tensor, "act": block.scalar, "dve": block.vector,
                      "pool": block.gpsimd, "sp": block.sync}
            ninst = 0
            nwait = 0
            for e in ENGS:
                ops = self.ops[e]
                if not ops:
                    continue

                def body(eng, ops=ops, e=e):
                    nonlocal ninst, nwait
                    seen = {}
                    for op in ops:
                        for d in op.deps:
                            key = id(d.sem)
                            if seen.get(key, 0) >= d.val:
                                continue
                            eng.wait_ge(d.sem, d.val)
                            nwait += 1
                            seen[key] = d.val
                        if op.fn is None:
                            continue
                        ins = op.fn(eng)
                        ninst += 1
                        if op.cc:
                            ins.then_inc(op.sem)
                        elif op.signal:
                            ins.then_inc(op.sem, 16 if op.dma else 1)

                engmap[e](body)
            self.stats = (ninst, nwait, self.sb_peak)


import math

NT, NL, NCX, D = 4352, 4096, 256, 1024
NTILE = 34
DEPTH = 4
ALPHA = (2 * DEPTH) ** 0.25
EPS = 1e-6
F_COLS = 512 + 128 + 512 + 512 + 1536 + 3072 + 32
T_COLS = 128 + 512 + 512
D_IN = 7456


def w_in_perm():
    src = {}
    o = 0
    for name, n in (("qa", 512), ("ka", 128), ("va", 128), ("qb", 512), ("kb", 512), ("vb", 512), ("qkvc", 1536),
                    ("gc", 512), ("a", 16), ("b", 16), ("mg", 3072)):
        src[name] = np.arange(o, o + n)
        o += n
    qa = src["qa"].reshape(8, 64)
    qa_p = np.concatenate([np.concatenate([qa[c], qa[c + 4]]) for c in range(4)])
    perm = np.concatenate([qa_p, src["ka"], src["qb"], src["kb"], src["qkvc"], src["mg"], src["a"], src["b"],
                           src["va"], src["vb"], src["gc"]])
    assert perm.shape[0] == D_IN
    return perm


def const_inputs():
    c = {}
    c["ident"] = np.eye(128, dtype=np.float32)
    blk = np.zeros((128, 128), np.float32)
    blk[:64, :64] = 1.0 / 64
    blk[64:, 64:] = 1.0 / 64
    c["blkmean"] = blk
    blk1 = (blk > 0).astype(np.float32)
    c["blkones"] = blk1
    S = np.zeros((64, 64), np.float32)
    for m in range(64):
        q = m // 16
        if q % 2 == 0:
            S[m, m + 16] = -1.0
        else:
            S[m, m - 16] = 1.0
    S2 = np.zeros((128, 128), np.float32)
    S2[:64, :64] = S
    S2[64:, 64:] = S
    c["rotT"] = np.ascontiguousarray(S2.T)
    t = np.arange(NL)
    rows = (t // 64).astype(np.float32)
    cols = (t % 64).astype(np.float32)
    inv = (10000.0 ** (-np.arange(16, dtype=np.float32) / 16)).astype(np.float32)
    ar = rows[None, :] * inv[:, None]
    ac = cols[None, :] * inv[:, None]
    C = np.concatenate([np.cos(ar), np.cos(ar), np.cos(ac), np.cos(ac)], 0).astype(np.float32)
    Sn = np.concatenate([np.sin(ar), np.sin(ar), np.sin(ac), np.sin(ac)], 0).astype(np.float32)
    c["ropeC"] = np.concatenate([C, C], 0)
    c["ropeS"] = np.concatenate([Sn, Sn], 0)
    return c


class Ctx:
    pass


def setup(P, nlayers, ext_weights=True):
    G = Ctx()
    G.P = P
    G.nl = nlayers
    ext = lambda n, s, dt=F32: P.dram(n, s, dt, kind="ExternalInput")
    G.xin = ext("xin", [NT, D])
    G.cvec = ext("cvec", [2, D])
    if ext_weights:
        G.w_mod = ext("w_mod", [DEPTH, D, 6144])
        G.w_in = ext("w_in", [DEPTH, D, D_IN])
    G.b_mod = ext("b_mod", [DEPTH, 6144])
    G.qk_gain = ext("qk_gain", [DEPTH, 2, 64])
    G.c_ident = ext("ident", [128, 128])
    G.c_blkmean = ext("blkmean", [128, 128])
    G.c_blkones = ext("blkones", [128, 128])
    G.c_rotT = ext("rotT", [128, 128])
    G.c_ropeC = ext("ropeC", [128, NL])
    G.c_ropeS = ext("ropeS", [128, NL])
    G.X = P.dram("X", [NT, D], F32, nbuf=NTILE)
    G.qaT = P.dram("qaT", [4, 128, NT], BF16, nbuf=4)
    G.kaT = P.dram("kaT", [128, NT], BF16)
    G.qbT = P.dram("qbT", [4, 128, NT], BF16, nbuf=4)
    G.kbT = P.dram("kbT", [4, 128, NT], BF16, nbuf=4)
    G.czT = P.dram("czT", [1536, NT], F32, nbuf=12)
    G.mgT = P.dram("mgT", [3072, NT], BF16, nbuf=24)
    G.abT = P.dram("abT", [32, NT], F32)
    G.VAa = P.dram("VAa", [NT, 2, 65], BF16)
    G.VBa = P.dram("VBa", [NT, 8, 65], BF16)
    G.sgC = P.dram("sgC", [NT, 512], BF16)
    G.ident = P.sb("ident", [128, 128], F32)
    P.dma("sp", G.ident.all(), G.c_ident.all())
    G.identb = P.sb("identb", [128, 128], BF16)
    P.copy("dve", G.identb.all(), G.ident.all())
    G.eps = P.sb("eps", [128, 1], F32)
    P.memset("dve", G.eps.all(), EPS)
    G.sc = P.sb("sc", [128, 8, 2], F32)
    craw = P.sb("craw", [128, 2, 8], F32)
    for j in range(2):
        P.dma("sp", craw[:, j, :], V(G.cvec.ap[j].rearrange("(k p) -> p k", p=128), G.cvec.bufs),
              allow_slow_non_contiguous=True)
    for j in range(2):
        P.act(G.sc[:, :, j], craw[:, j, :], AF.Silu)
    G.sc_rep = P.sb("sc_rep", [128, 8, 2, 128], F32)
    for k in range(8):
        for j in range(2):
            P.copy("dve", G.sc_rep[:, k, j, :], G.sc[:, k, j:j + 1].bc([128, 128]))
    G.modT = P.sb("modT", [128, 48, 2], F32)
    G.mod1p = P.sb("mod1p", [128, 48, 2], F32)
    G.modbc = P.sb("modbc", [128, 4, 2, 1024], F32)
    return G


def stage_mod(G, l):
    P = G.P
    m = P.mark()
    bT = P.sb("bT", [128, 48], F32)
    P.dma("sp", bT.all(), V(G.b_mod.ap[l].rearrange("(j p) -> p j", p=128), G.b_mod.bufs), allow_slow_non_contiguous=True)
    bbc = P.sb("bbc", [128, 6144], F32)
    P.dma("sp", bbc.all(), V(G.b_mod.ap[l].partition_broadcast(128), G.b_mod.bufs))
    wb = [P.sb(f"wmodblk{i}", [128, 8, 512], F32) for i in range(2)]
    for blk in range(12):
        w = wb[blk % 2]
        P.dma("sp", w.all(), V(G.w_mod.ap[l, :, blk * 512:(blk + 1) * 512].rearrange("(k p) c -> p k c", p=128), G.w_mod.bufs))
        ps = P.psum[blk % 2]
        for c4 in range(4):
            for k in range(8):
                P.matmul(ps[:, c4 * 2:c4 * 2 + 2], w[:, k, c4 * 128:(c4 + 1) * 128], G.sc[:, k, :], start=(k == 0), stop=(k == 7))
        for j in range(2):
            P.ts("dve", G.modT[:, blk * 4:(blk + 1) * 4, j], ps[:, 0:8].rearrange("p (c w) -> p c w", w=2)[:, :, j],
                 1.0, ALU.mult)
        g = blk // 2
        if g >= 2:
            for j in range(2):
                pb = P.psum[2 + j]
                for k in range(8):
                    P.matmul(pb.all(), G.sc_rep[:, k, j, :], w[:, k, :], start=(k == 0), stop=(k == 7))
                half = blk % 2
                P.tt("dve", G.modbc[:, g - 2, j, half * 512:(half + 1) * 512], pb.all(), bbc[:, blk * 512:(blk + 1) * 512], ALU.add)
    for j in range(2):
        P.tt("dve", G.modT[:, :, j], G.modT[:, :, j], bT.all(), ALU.add)
    P.ts("dve", G.mod1p.all(), G.modT.all(), 1.0, ALU.add)
    P.ts("dve", G.modbc[:, 2, :, :], G.modbc[:, 2, :, :], 1.0, ALU.add)
    P.release(m)


def ln_stats(P, xt, tmp):
    st = tmp["st"]
    mv = tmp["mv"]
    rstd = tmp["rstd"]
    for c in range(2):
        P.generic("dve", lambda e, c=c: e.bn_stats(st.ap[:, c, :], xt.ap[:, c * 512:(c + 1) * 512]), [xt.all()], [st.all()])
    P.generic("dve", lambda e: e.bn_aggr(mv.ap, st.ap), [st.all()], [mv.all()])
    return mv


def stage_ln_T(G, src, hT):
    P = G.P
    m = P.mark()
    xts = [P.sb(f"lnx{i}", [128, 1024], F32) for i in range(2)]
    xns = [P.sb(f"lnxn{i}", [128, 1024], F32) for i in range(2)]
    tmp = [dict(st=P.sb("st", [128, 2, 6], F32), mv=P.sb("mv", [128, 2], F32), rstd=P.sb("rstd", [128, 1], F32),
                sd=P.sb("sd", [128, 1], F32)) for i in range(2)]
    for t in range(NTILE):
        xt, xn, tm = xts[t % 2], xns[t % 2], tmp[t % 2]
        w = 0 if t < 32 else 1
        P.dma("sp", xt.all(), src.at(t, (slice(t * 128, (t + 1) * 128), slice(None))))
        mv = ln_stats(P, xt, tm)
        P.act(tm["sd"].all(), mv[:, 1:2], AF.Sqrt, bias=G.eps.all(), scale=1.0)
        P.recip(tm["rstd"].all(), tm["sd"].all())
        P.ts("dve", xn.all(), xt.all(), mv[:, 0:1], ALU.subtract, tm["rstd"].all(), ALU.mult)
        for hf in range(2):
            ps = P.psum[(t * 2 + hf) % 4]
            for k4 in range(4):
                k = hf * 4 + k4
                P.transpose(ps[:, k4 * 128:(k4 + 1) * 128], xn[:, k * 128:(k + 1) * 128], G.ident.all())
            for k4 in range(4):
                k = hf * 4 + k4
                if k4 % 2 == 0:
                    P.act(hT.at(t, (slice(None), k, slice(t * 128, (t + 1) * 128))), ps[:, k4 * 128:(k4 + 1) * 128], AF.Identity,
                          bias=G.modT[:, 0 * 8 + k, w:w + 1], scale=G.mod1p[:, 1 * 8 + k, w:w + 1])
                else:
                    P.ts("dve", hT.at(t, (slice(None), k, slice(t * 128, (t + 1) * 128))), ps[:, k4 * 128:(k4 + 1) * 128],
                         G.mod1p[:, 1 * 8 + k, w:w + 1], ALU.mult, G.modT[:, 0 * 8 + k, w:w + 1], ALU.add)
    P.release(m)


TOK_CHUNKS = [(i * 512, 512) for i in range(8)] + [(4096, 256)]


def stage_inproj(G, l, hT):
    P = G.P
    m = P.mark()
    blkmean = P.sb("blkmean", [128, 128], F32)
    P.dma("sp", blkmean.all(), G.c_blkmean.all())
    rotT = P.sb("rotT", [128, 128], F32)
    P.dma("sp", rotT.all(), G.c_rotT.all())
    rc_ = [P.sb(f"ropeC{i}", [128, 512], F32) for i in range(2)]
    rs_ = [P.sb(f"ropeS{i}", [128, 512], F32) for i in range(2)]
    gain = P.sb("gain", [128, 2], F32)
    for j in range(2):
        for hf in range(2):
            P.dma("sp", gain[hf * 64:(hf + 1) * 64, j:j + 1], V(G.qk_gain.ap[l, j].rearrange("(d o) -> d o", o=1), G.qk_gain.bufs),
                  allow_slow_non_contiguous=True)
    wbl = [P.sb(f"winblk{i}", [128, 8, 512], BF16) for i in range(3)]
    NW = 2
    zs_ = [P.sb(f"zs{i}", [128, 512], F32) for i in range(NW)]
    sq_ = [P.sb(f"sq{i}", [128, 512], F32) for i in range(NW)]
    xn_ = [P.sb(f"xn{i}", [128, 512], F32) for i in range(NW)]
    t1_ = [P.sb(f"t1{i}", [128, 512], F32) for i in range(NW)]
    ob_ = [P.sb(f"ob{i}", [128, 512], BF16) for i in range(NW)]
    of_ = [P.sb(f"of{i}", [128, 512], F32) for i in range(NW)]
    unit = 0
    nblk = (F_COLS + 511) // 512
    for blk in range(nblk):
        c0 = blk * 512
        ncol = min(512, F_COLS - c0)
        w = wbl[blk % 3]
        P.dma("pool", w[:, :, 0:ncol], V(G.w_in.ap[l, :, c0:c0 + ncol].rearrange("(k p) c -> p k c", p=128), G.w_in.bufs))
        for cc in range((ncol + 127) // 128):
            col = c0 + cc * 128
            nr = min(128, F_COLS - col)
            chunk = col // 128
            for (t0, tn) in TOK_CHUNKS:
                u = unit % NW
                unit += 1
                ps = P.psum[unit % 3]
                tl = list(range(t0 // 128, (t0 + tn) // 128))
                for k in range(8):
                    P.matmul(ps[0:nr, 0:tn], w[:, k, cc * 128:cc * 128 + nr], hT.at(tl, (slice(None), k, slice(t0, t0 + tn))),
                             start=(k == 0), stop=(k == 7))
                zs, sq, xn, t1, ob, of = zs_[u], sq_[u], xn_[u], t1_[u], ob_[u], of_[u]
                if chunk < 5:
                    gj = 0 if chunk < 4 else 1
                    P.copy("act", zs[:, 0:tn], ps[:, 0:tn])
                    P.act(sq[:, 0:tn], ps[:, 0:tn], AF.Square)
                    ps2 = P.psum[3 + unit % 2]
                    P.matmul(ps2[:, 0:tn], blkmean.all(), sq[:, 0:tn])
                    P.act(sq[:, 0:tn], ps2[:, 0:tn], AF.Sqrt, bias=G.eps.all(), scale=1.0)
                    P.recip(sq[:, 0:tn], sq[:, 0:tn])
                    P.stt("dve", xn[:, 0:tn], zs[:, 0:tn], gain[:, gj:gj + 1], sq[:, 0:tn], ALU.mult, ALU.mult)
                    if t0 < NL:
                        ropeC, ropeS = rc_[unit % 2], rs_[unit % 2]
                        P.dma("sp", ropeC.all(), G.c_ropeC[:, t0:t0 + tn])
                        P.dma("sp", ropeS.all(), G.c_ropeS[:, t0:t0 + tn])
                        ps3 = P.psum[5 + unit % 2]
                        P.matmul(ps3[:, 0:tn], rotT.all(), xn[:, 0:tn])
                        P.tt("pool", t1[:, 0:tn], xn[:, 0:tn], ropeC[:, 0:tn], ALU.mult)
                        P.tt("dve", zs[:, 0:tn], ps3[:, 0:tn], ropeS[:, 0:tn], ALU.mult)
                        P.tt("dve", ob[:, 0:tn], t1[:, 0:tn], zs[:, 0:tn], ALU.add)
                    else:
                        P.copy("dve", ob[:, 0:tn], xn[:, 0:tn])
                    if chunk < 4:
                        P.dma("sp", G.qaT.at(chunk, (chunk, slice(None), slice(t0, t0 + tn))), ob[:, 0:tn])
                    else:
                        P.dma("sp", G.kaT[:, t0:t0 + tn], ob[:, 0:tn])
                elif chunk < 13:
                    P.copy("act", ob[:, 0:tn], ps[:, 0:tn])
                    if chunk < 9:
                        P.dma("sp", G.qbT.at(chunk - 5, (chunk - 5, slice(None), slice(t0, t0 + tn))), ob[:, 0:tn])
                    else:
                        P.dma("sp", G.kbT.at(chunk - 9, (chunk - 9, slice(None), slice(t0, t0 + tn))), ob[:, 0:tn])
                elif chunk < 25:
                    j = chunk - 13
                    P.copy("act", of[:, 0:tn], ps[:, 0:tn])
                    P.dma("sp", G.czT.at(j, (slice(j * 128, (j + 1) * 128), slice(t0, t0 + tn))), of[:, 0:tn])
                elif chunk < 49:
                    j = chunk - 25
                    P.act(ob[:, 0:tn], ps[:, 0:tn], AF.Sigmoid)
                    P.dma("sp", G.mgT.at(j, (slice(j * 128, (j + 1) * 128), slice(t0, t0 + tn))), ob[:, 0:tn])
                else:
                    P.copy("act", of[0:32, 0:tn], ps[0:32, 0:tn])
                    P.dma("sp", G.abT[:, t0:t0 + tn], of[0:32, 0:tn])
    P.release(m)
    m = P.mark()
    wt = P.sb("wint", [128, 8, T_COLS], BF16)
    P.dma("pool", wt.all(), V(G.w_in.ap[l, :, F_COLS:D_IN].rearrange("(k p) c -> p k c", p=128), G.w_in.bufs))
    va_ = [P.sb(f"va{i}", [128, 2, 65], BF16) for i in range(2)]
    vb_ = [P.sb(f"vb{i}", [128, 8, 65], BF16) for i in range(2)]
    sg_ = [P.sb(f"sg{i}", [128, 512], BF16) for i in range(2)]
    for i in range(2):
        P.memset("dve", va_[i].all(), 1.0)
        P.memset("dve", vb_[i].all(), 1.0)
    for t in range(NTILE):
        tsl = slice(t * 128, (t + 1) * 128)
        va, vb, sg = va_[t % 2], vb_[t % 2], sg_[t % 2]
        pa, pb, pg = P.psum[0 + 3 * (t % 2)], P.psum[1 + 3 * (t % 2)], P.psum[2 + 3 * (t % 2)]
        for (ps, c0, n) in ((pa, 0, 128), (pb, 128, 512), (pg, 640, 512)):
            for k in range(8):
                P.matmul(ps[:, 0:n], hT.at(t, (slice(None), k, tsl)), wt[:, k, c0:c0 + n], start=(k == 0), stop=(k == 7))
        P.copy("act", va[:, :, 0:64], pa[:, 0:128].rearrange("p (h d) -> p h d", d=64))
        P.copy("dve", vb[:, :, 0:64], pb[:, 0:512].rearrange("p (h d) -> p h d", d=64))
        P.act(sg.all(), pg.all(), AF.Silu)
        P.dma("sp", G.VAa[tsl], va.all())
        P.dma("sp", G.VBa[tsl], vb.all())
        P.dma("sp", G.sgC[tsl], sg.all())
    P.release(m)


SCALE = 0.125


def attn_setup(G):
    P = G.P
    G.oaT = P.dram("oaT", [512, NT], BF16, nbuf=8)
    G.obT = P.dram("obT", [512, NT], BF16, nbuf=8)
    G.ocT = P.dram("ocT", [512, NT], BF16, nbuf=8)


def attn_unit(G, st, qv, kfn, vfn, key_tiles, nq, outv):
    P = G.P
    i = st["i"]
    st["i"] += 1
    po = P.psum[4 + i % 2]
    n = len(key_tiles)
    for j, kt in enumerate(key_tiles):
        ps = P.psum[st["j"] % 4]
        pt = st["pt"][st["j"] % 3]
        st["j"] += 1
        P.matmul(ps[:, 0:nq], kfn(kt), qv)
        P.act(pt[:, 0:nq], ps[:, 0:nq], AF.Exp, scale=SCALE)
        P.matmul(po[0:65, 0:nq], vfn(kt), pt[:, 0:nq], start=(j == 0), stop=(j == n - 1))
    st["i"] -= 1
    attn_epilogue(G, st, po, nq, outv)


def attn_state(G):
    P = G.P
    st = {"i": 0, "j": 0}
    st["pt"] = [P.sb(f"pt{i}", [128, 512], BF16) for i in range(3)]
    st["rr"] = [P.sb(f"rr{i}", [128, 512], F32) for i in range(2)]
    st["bcs"] = [P.sb(f"bcs{i}", [128, 512], F32) for i in range(2)]
    st["ot"] = [P.sb(f"ot{i}", [128, 512], BF16) for i in range(2)]
    st["ones"] = P.sb("ones", [128, 64], F32)
    P.memset("dve", st["ones"].all(), 1.0)
    return st


def stage_attnA(G, need_ctx=True):
    P = G.P
    m = P.mark()
    st = attn_state(G)
    kT = P.sb("kaT", [128, NT], BF16)
    P.dma("sp", kT.all(), G.kaT.all())
    va = P.sb("vaa", [128, NTILE, 130], BF16)
    P.dma("sp", va.all(), V(G.VAa.ap.rearrange("(t p) g e -> p t (g e)", p=128), G.VAa.bufs))
    qs = [P.sb(f"qaT{i}", [128, NT], BF16) for i in range(2)]
    for c in range(4):
        q = qs[c % 2]
        P.dma("sp", q.all(), G.qaT.at(c, (c,)))
        for g in range(2):
            h = c + 4 * g
            psl = slice(g * 64, (g + 1) * 64)
            kfn = lambda kt: kT[psl, kt * 128:(kt + 1) * 128]
            vfn = lambda kt: va[:, kt, g * 65:(g + 1) * 65]
            for (t0, tn) in TOK_CHUNKS:
                if t0 < NL:
                    kts = list(range(NTILE))
                else:
                    if not need_ctx:
                        continue
                    kts = [32, 33]
                attn_unit(G, st, q[psl, t0:t0 + tn], kfn, vfn, kts, tn,
                          G.oaT.at(h, (slice(h * 64, (h + 1) * 64), slice(t0, t0 + tn))))
    P.release(m)


def rpb_gather_host(rpb):
    kc = np.arange(64)[:, None]
    qc = np.arange(64)[None, :]
    idx = np.clip(kc - qc + 15, 0, 30)
    return np.ascontiguousarray(rpb[:, :, :, idx]).astype(np.float32)


def maskB_const():
    qc = np.arange(64)
    cs = np.clip(qc - 8, 0, 48)
    kc = np.arange(64)[:, None]
    inw = (kc >= cs[None, :]) & (kc < cs[None, :] + 16)
    m = np.where(inw, 0.0, -30000.0).astype(np.float32)
    return np.concatenate([m, m], 0)


def attn_epilogue(G, st, po, nq, outv):
    P = G.P
    i = st["i"]
    st["i"] += 1
    rr = st["rr"][i % 2]
    bcs = st["bcs"][i % 2]
    ot = st["ot"][i % 2]
    P.copy("act", rr[64:65, 0:nq], po[64:65, 0:nq])
    P.recip(rr[64:65, 0:nq], rr[64:65, 0:nq])
    pb = P.psum[6 + i % 2]
    P.matmul(pb[0:64, 0:nq], st["ones"][64:65, 0:64], rr[64:65, 0:nq])
    P.copy("act", bcs[0:64, 0:nq], pb[0:64, 0:nq])
    P.tt("dve", ot[0:64, 0:nq], po[0:64, 0:nq], bcs[0:64, 0:nq], ALU.mult)
    P.dma("sp", outv, ot[0:64, 0:nq])


def stage_attnB(G, l, need_ctx=True):
    P = G.P
    m = P.mark()
    st = attn_state(G)
    mk = P.sb("maskB", [128, 64], F32)
    P.dma("sp", mk.all(), G.c_maskB.all())
    MP = P.sb("MP", [128, 8, 14, 64], BF16, nbuf=8)
    r1 = [P.sb(f"rpb1_{i}", [128, 14, 64], F32) for i in range(2)]
    for h in range(8):
        r = r1[h % 2]
        P.dma("sp", r[0:64], V(G.rpbg.ap[l, h, 0:14].rearrange("o k q -> k o q"), G.rpbg.bufs))
        P.dma("sp", r[64:128], V(G.rpbg.ap[l, h, 1:15].rearrange("o k q -> k o q"), G.rpbg.bufs))
        P.tt("dve", r.all(), r.all(), mk[:, None, :].bc([128, 14, 64]), ALU.add)
        P.act(MP.at(h, (slice(None), h)), r.all(), AF.Exp)
    vb0 = P.sb("vb0", [128, 32, 520], BF16)
    vb1 = P.sb("vb1", [128, 31, 520], BF16)
    vbx = P.sb("vbx", [128, 2, 520], BF16)
    P.dma("sp", vb0.all(), V(G.VBa.ap[0:4096].rearrange("(t p) h e -> p t (h e)", p=128), G.VBa.bufs))
    P.dma("sp", vb1.all(), V(G.VBa.ap[64:64 + 31 * 128].rearrange("(t p) h e -> p t (h e)", p=128), G.VBa.bufs))
    P.dma("sp", vbx.all(), V(G.VBa.ap[4096:NT].rearrange("(t p) h e -> p t (h e)", p=128), G.VBa.bufs))
    qs = [P.sb(f"qbT{i}", [128, NT], BF16) for i in range(2)]
    ks = [P.sb(f"kbT{i}", [128, NT], BF16) for i in range(2)]
    pts = [P.sb(f"ptb{i}", [128, 384], BF16) for i in range(3)]
    nb = 0
    for c in range(4):
        q, k = qs[c % 2], ks[c % 2]
        P.dma("sp", q.all(), G.qbT.at(c, (c,)))
        P.dma("sp", k.all(), G.kbT.at(c, (c,)))
        for half in range(2):
            h = 2 * c + half
            psl = slice(half * 64, (half + 1) * 64)
            po = None
            for r in range(64):
                rs = min(max(r - 4, 0), 56)
                o = rs - r + 7
                if r % 8 == 0:
                    po = P.psum[4 + (r // 8) % 2]
                ps = P.psum[nb % 4]
                pt = pts[nb % 3]
                nb += 1
                for t in range(6):
                    k0 = (rs + 2 * t) * 64 if t < 4 else 4096 + (t - 4) * 128
                    P.matmul(ps[:, t * 64:(t + 1) * 64], k[psl, k0:k0 + 128], q[psl, r * 64:(r + 1) * 64])
                P.act(pt.all(), ps[:, 0:384], AF.Exp, scale=SCALE)
                P.tt("dve", pt[:, 0:256].rearrange("p (t q) -> p t q", q=64), pt[:, 0:256].rearrange("p (t q) -> p t q", q=64),
                     MP.at(h, (slice(None), h, slice(o, o + 7, 2), slice(None))), ALU.mult)
                for t in range(6):
                    if t < 4:
                        kr = rs + 2 * t
                        vt = vb0[:, kr // 2, h * 65:(h + 1) * 65] if kr % 2 == 0 else vb1[:, (kr - 1) // 2, h * 65:(h + 1) * 65]
                    else:
                        vt = vbx[:, t - 4, h * 65:(h + 1) * 65]
                    P.matmul(po[0:65, (r % 8) * 64:(r % 8 + 1) * 64], vt, pt[:, t * 64:(t + 1) * 64], start=(t == 0), stop=(t == 5))
                if r % 8 == 7:
                    t0 = (r - 7) * 64
                    attn_epilogue(G, st, po, 512, G.obT.at(h, (slice(h * 64, (h + 1) * 64), slice(t0, t0 + 512))))
            if need_ctx:
                kfn = lambda kt: k[psl, kt * 128:(kt + 1) * 128]
                vfn = lambda kt: vbx[:, kt - 32, h * 65:(h + 1) * 65]
                attn_unit(G, st, q[psl, 4096:NT], kfn, vfn, [32, 33], 256,
                          G.obT.at(h, (slice(h * 64, (h + 1) * 64), slice(4096, NT))))
    P.release(m)


def setupB(G):
    P = G.P
    G.c_maskB = P.dram("maskB", [128, 64], F32, kind="ExternalInput")
    G.rpbg = P.dram("rpbg", [DEPTH, 8, 15, 64, 64], F32, kind="ExternalInput")


def setupM(G, ext_weights=True):
    P = G.P
    ext = lambda n, s, dt=F32: P.dram(n, s, dt, kind="ExternalInput")
    if ext_weights:
        G.w_branch = ext("w_branch", [DEPTH, 3, 512, 1024])
        G.w_out = ext("w_out", [DEPTH, 1024, 1024])
    G.ln1_g = ext("ln1_g", [DEPTH, 1024])
    G.ln1_b = ext("ln1_b", [DEPTH, 1024])
    G.ln2_g = ext("ln2_g", [DEPTH, 1024])
    G.ln2_b = ext("ln2_b", [DEPTH, 1024])


def res_ln_tiles(P):
    return dict(st=P.sb("st", [128, 2, 6], F32), mv=P.sb("mv", [128, 2], F32), rstd=P.sb("rstd", [128, 1], F32),
                sd=P.sb("sd", [128, 1], F32), y=P.sb("y", [128, 1024], F32), x=P.sb("x", [128, 1024], F32))


def res_ln(G, tm, upd_halves, gate_bc, g_bc, b_bc, xsrc, dst):
    P = G.P
    x, y = tm["x"], tm["y"]
    P.dma("sp", x.all(), xsrc)
    for hf in range(2):
        P.tt("dve", y[:, hf * 512:(hf + 1) * 512], upd_halves[hf], gate_bc[:, hf * 512:(hf + 1) * 512], ALU.mult)
    P.stt("dve", y.all(), x.all(), ALPHA, y.all(), ALU.mult, ALU.add)
    mv = ln_stats(P, y, tm)
    P.act(tm["sd"].all(), mv[:, 1:2], AF.Sqrt, bias=G.eps.all(), scale=1.0)
    P.recip(tm["rstd"].all(), tm["sd"].all())
    P.ts("dve", x.all(), y.all(), mv[:, 0:1], ALU.subtract, tm["rstd"].all(), ALU.mult)
    P.tt("pool", x.all(), x.all(), g_bc.all(), ALU.mult)
    P.tt("pool", y.all(), x.all(), b_bc.all(), ALU.add)
    P.dma("sp", dst, y.all())


def stage_merge(G, l, xsrc, need_ctx=True):
    P = G.P
    m = P.mark()
    wbr = P.sb("wbr", [128, 3, 4, 1024], BF16)
    for i in range(3):
        P.dma("pool", wbr[:, i], V(G.w_branch.ap[l, i].rearrange("(k p) c -> p k c", p=128), G.w_branch.bufs))
    wout = P.sb("wout", [128, 8, 1024], BF16)
    P.dma("pool", wout.all(), V(G.w_out.ap[l].rearrange("(k p) c -> p k c", p=128), G.w_out.bufs))
    g_bc = P.sb("g_bc", [128, 1024], F32)
    b_bc = P.sb("b_bc", [128, 1024], F32)
    P.dma("sp", g_bc.all(), V(G.ln1_g.ap[l].partition_broadcast(128), G.ln1_g.bufs))
    P.dma("sp", b_bc.all(), V(G.ln1_b.ap[l].partition_broadcast(128), G.ln1_b.bufs))
    ob3 = [P.sb(f"ob3_{i}", [128, 3, 4, 512], BF16) for i in range(2)]
    mgs = [P.sb(f"mgs{i}", [128, 24, 512], BF16) for i in range(2)]
    mT = [P.sb(f"mT{i}", [128, 8, 512], BF16) for i in range(2)]
    ta = [P.sb(f"ta{i}", [128, 512], F32) for i in range(2)]
    tb = [P.sb(f"tb{i}", [128, 512], F32) for i in range(2)]
    tms = [res_ln_tiles(P) for i in range(2)]
    srcs = (G.oaT, G.obT, G.ocT)
    nt = 0
    for ci, (t0, tn) in enumerate(TOK_CHUNKS):
        if t0 >= NL and not need_ctx:
            continue
        o3, mg, mt = ob3[ci % 2], mgs[ci % 2], mT[ci % 2]
        for i in range(3):
            P.dma("sp", o3[:, i, :, 0:tn], V(srcs[i].ap[:, t0:t0 + tn].rearrange("(k p) t -> p k t", p=128), srcs[i].bufs))
        P.dma("sp", mg[:, :, 0:tn], V(G.mgT.ap[:, t0:t0 + tn].rearrange("(j p) t -> p j t", p=128), G.mgT.bufs))
        for fc in range(8):
            pss = [P.psum[(fc % 2) * 3 + i] for i in range(3)]
            for i in range(3):
                for k in range(4):
                    P.matmul(pss[i][:, 0:tn], wbr[:, i, k, fc * 128:(fc + 1) * 128], o3[:, i, k, 0:tn], start=(k == 0), stop=(k == 3))
            a, b = ta[fc % 2], tb[fc % 2]
            P.tt("dve", a[:, 0:tn], pss[0][:, 0:tn], mg[:, fc, 0:tn], ALU.mult)
            P.tt("dve", b[:, 0:tn], pss[1][:, 0:tn], mg[:, 8 + fc, 0:tn], ALU.mult)
            P.tt("pool", a[:, 0:tn], a[:, 0:tn], b[:, 0:tn], ALU.add)
            P.tt("dve", b[:, 0:tn], pss[2][:, 0:tn], mg[:, 16 + fc, 0:tn], ALU.mult)
            P.tt("pool", mt[:, fc, 0:tn], a[:, 0:tn], b[:, 0:tn], ALU.add)
        for tt in range(tn // 128):
            t = (t0 // 128) + tt
            w = 0 if t < 32 else 1
            tm = tms[nt % 2]
            pp = [P.psum[6], P.psum[7]]
            for hf in range(2):
                for fc in range(8):
                    P.matmul(pp[hf].all(), mt[:, fc, tt * 128:(tt + 1) * 128], wout[:, fc, hf * 512:(hf + 1) * 512],
                             start=(fc == 0), stop=(fc == 7))
            nt += 1
            rows = slice(t * 128, (t + 1) * 128)
            res_ln(G, tm, [pp[0].all(), pp[1].all()], G.modbc[:, 0, w, :], g_bc, b_bc,
                   xsrc.at(t, (rows, slice(None))), G.X.at(t, (rows, slice(None))))
    P.release(m)


NCH = 34


def gdn_consts():
    p = np.arange(128)[:, None]
    f = np.arange(128)[None, :]
    tri = np.stack([(p >= f), (p <= f), (p > f), (p < f)]).astype(np.float32)
    dm = np.zeros((16, 2), np.float32)
    dm[:8, 0] = 1.0
    dm[8:, 1] = 1.0
    rm = np.ones((16, NT), np.float32)
    rm[:, ::128] = 0.0
    return {"gtri": tri, "gdirmask": dm, "gresetm": rm}


def setupC(G):
    P = G.P
    ext = lambda n, s, dt=F32: P.dram(n, s, dt, kind="ExternalInput")
    G.conv_w = ext("conv_w", [DEPTH, 4, 1536])
    G.a_log = ext("a_log", [DEPTH, 16])
    G.dt_bias = ext("dt_bias", [DEPTH, 16])
    G.o_gain = ext("o_gain", [DEPTH, 64])
    G.c_gtri = ext("gtri", [4, 128, 128])
    G.c_gdirmask = ext("gdirmask", [16, 2])
    G.c_gresetm = ext("gresetm", [16, NT])
    G.cT = P.dram("cT", [1536, NT], F32, nbuf=12)
    G.GC = P.dram("GC", [16, NT], F32)
    G.BETA = P.dram("BETA", [16, NT], F32)
    G.GL = P.dram("GL", [16, NCH], F32)
    G.dU = P.dram("dU", [NCH, 128, 16, 64], F32, nbuf=NCH)
    G.dKD = P.dram("dKD", [NCH, 128, 16, 64], F32, nbuf=NCH)
    G.dWT = P.dram("dWT", [NCH, 128, 8, 128], F32, nbuf=NCH)
    G.dQD = P.dram("dQD", [NCH, 128, 8, 128], F32, nbuf=NCH)
    G.dIT = P.dram("dIT", [NCH, 128, 16, 128], F32, nbuf=NCH)
    G.O2 = P.dram("O2", [2, NT, 512], F32, nbuf=2)


def stage_gdn_conv(G, l):
    P = G.P
    m = P.mark()
    cw = P.sb("cw", [128, 12, 4], F32)
    for j in range(4):
        P.dma("sp", cw[:, :, j], V(G.conv_w.ap[l, j].rearrange("(c p) -> p c", p=128), G.conv_w.bufs), allow_slow_non_contiguous=True)
    blk1 = P.sb("blk1", [128, 128], F32)
    P.dma("sp", blk1.all(), G.c_blkones.all())
    xts = [P.sb(f"cx{i}", [128, 1024 + 3], F32) for i in range(2)]
    accs = [P.sb(f"cacc{i}", [128, 1024], F32) for i in range(2)]
    sqs = [P.sb(f"csq{i}", [128, 1024], F32) for i in range(2)]
    rns = [P.sb(f"crn{i}", [128, 512], F32) for i in range(2)]
    blocks = [(i * 1024, 1024, 0, NL) for i in range(4)] + [(NL, 256, NL, NT)]
    it = 0
    for cc in range(12):
        rows = slice(cc * 128, (cc + 1) * 128)
        for (t0, tn, s0, s1) in blocks:
            xt, acc, sq = xts[it % 2], accs[it % 2], sqs[it % 2]
            it += 1
            lo = max(t0 - 2, s0)
            hi = min(t0 + tn + 1, s1)
            if lo > t0 - 2:
                P.memset("pool", xt[:, 0:2], 0.0)
            if hi < t0 + tn + 1:
                P.memset("pool", xt[:, tn + 2:tn + 3], 0.0)
            P.dma("sp", xt[:, lo - (t0 - 2):hi - (t0 - 2)], G.czT.at(cc, (rows, slice(lo, hi))))
            P.ts("dve", acc[:, 0:tn], xt[:, 0:tn], cw[:, cc, 0:1], ALU.mult)
            for j in range(1, 4):
                P.stt("dve", acc[:, 0:tn], xt[:, j:j + tn], cw[:, cc, j:j + 1], acc[:, 0:tn], ALU.mult, ALU.add)
            P.act(acc[:, 0:tn], acc[:, 0:tn], AF.Silu)
            if cc < 8:
                P.tt("pool", sq[:, 0:tn], acc[:, 0:tn], acc[:, 0:tn], ALU.mult)
                for sb in range((tn + 511) // 512):
                    n = min(512, tn - sb * 512)
                    sl = slice(sb * 512, sb * 512 + n)
                    ps = P.psum[(it + sb) % 4]
                    rn = rns[sb % 2]
                    P.matmul(ps[:, 0:n], blk1.all(), sq[:, sl])
                    P.act(rn[:, 0:n], ps[:, 0:n], AF.Sqrt, bias=G.eps.all(), scale=1.0)
                    P.recip(rn[:, 0:n], rn[:, 0:n])
                    P.stt("dve", acc[:, sl], acc[:, sl], 0.125 if cc < 4 else 1.0, rn[:, 0:n], ALU.mult, ALU.mult)
            P.dma("sp", G.cT.at(cc, (rows, slice(t0, t0 + tn))), acc[:, 0:tn])
    P.release(m)


def stage_gdn_gates(G, l, S):
    P = G.P
    m = P.mark()
    a_t = P.sb("a_t", [16, NT], F32)
    b_t = P.sb("b_t", [16, NT], F32)
    P.dma("sp", a_t.all(), G.abT[0:16, :])
    P.dma("sp", b_t.all(), G.abT[16:32, :])
    sc = P.sb("gsc", [16, 8], F32)
    P.dma("sp", sc[:, 0:1], V(G.a_log.ap[l].rearrange("(c o) -> c o", o=1), G.a_log.bufs), allow_slow_non_contiguous=True)
    P.dma("sp", sc[:, 1:2], V(G.dt_bias.ap[l].rearrange("(c o) -> c o", o=1), G.dt_bias.bufs), allow_slow_non_contiguous=True)
    P.dma("sp", sc[:, 4:6], G.c_gdirmask.all())
    P.memset("dve", sc[:, 3:4], 1.0)
    P.act(sc[:, 2:3], sc[:, 0:1], AF.Exp)
    P.ts("dve", sc[:, 2:3], sc[:, 2:3], -1.0, ALU.mult)
    rm = P.sb("grm", [16, NT], F32)
    P.dma("sp", rm.all(), G.c_gresetm.all())
    pre = P.sb("gpre", [16, NT], F32)
    gl = P.sb("ggl", [16, NCH], F32)
    suf = rm
    for (c0, n) in [(k * 1024, 1024) for k in range(4)] + [(4096, 256)]:
        sl = slice(c0, c0 + n)
        nch = n // 128
        ch = slice(c0 // 128, c0 // 128 + nch)
        P.act(a_t[:, sl], a_t[:, sl], AF.Exp, bias=sc[:, 1:2], scale=1.0)
        P.act(a_t[:, sl], a_t[:, sl], AF.Ln, bias=sc[:, 3:4], scale=1.0)
        P.ts("dve", a_t[:, sl], a_t[:, sl], sc[:, 2:3], ALU.mult)
        P.act(b_t[:, sl], b_t[:, sl], AF.Sigmoid)
        P.generic("dve", lambda e, sl=sl: e.tensor_tensor_scan(pre.ap[:, sl], rm.ap[:, sl], a_t.ap[:, sl], 0.0, ALU.mult, ALU.add),
                  [rm.all(), a_t.all()], [pre.all()])
        pre3 = pre[:, sl].rearrange("c (n t) -> c n t", t=128)
        P.copy("dve", gl[:, ch], pre3[:, :, 127])
        P.tt("dve", suf[:, sl].rearrange("c (n t) -> c n t", t=128), gl[:, ch][:, :, None].bc([16, nch, 128]), pre3, ALU.subtract)
        P.tt("dve", suf[:, sl], suf[:, sl], a_t[:, sl], ALU.add)
        P.ts("dve", pre[:, sl], pre[:, sl], sc[:, 4:5], ALU.mult)
        P.stt("dve", pre[:, sl], suf[:, sl], sc[:, 5:6], pre[:, sl], ALU.mult, ALU.add)
    P.dma("sp", G.BETA.all(), b_t.all())
    P.dma("sp", G.GL.all(), gl.all())
    P.dma("sp", G.GC.all(), pre.all())
    for (src, dst) in ((pre, S["gccol"]), (b_t, S["betacol"])):
        for grp in range(2):
            ps = P.psum[grp]
            for i in range(17):
                n = grp * 17 + i
                P.transpose(ps[:, i * 16:(i + 1) * 16], src[:, n * 128:(n + 1) * 128], G.ident[0:16, 0:16])
            P.copy("act", dst[:, grp * 17:(grp + 1) * 17, :], ps[:, 0:272].rearrange("p (n c) -> p n c", c=16))
    P.release(m)
    P.dma("sp", S["GLb"].all(), V(G.GL.ap.partition_broadcast(128), G.GL.bufs))
    P.act(S["SD"].all(), S["GLb"].all(), AF.Exp)


def gdn_persist(G):
    P = G.P
    S = {}
    S["gccol"] = P.sb("gccol", [128, NCH, 16], F32)
    S["betacol"] = P.sb("betacol", [128, NCH, 16], F32)
    S["GLb"] = P.sb("GLb", [128, 16, NCH], F32)
    S["SD"] = P.sb("SD", [128, 16, NCH], F32)
    return S


def stage_gdn_prep(G, l, S, chunks):
    import os
    CUT = int(os.environ.get('CUT', '99'))
    P = G.P
    m = P.mark()
    tri = P.sb("tri", [128, 4, 128], F32)
    P.dma("sp", tri.all(), V(G.c_gtri.ap.rearrange("k p f -> p k f"), G.c_gtri.bufs))
    W = [128, 16, 128]
    mSA = P.sb("mSA", W, BF16)
    mST = P.sb("mST", W, BF16)
    mIT = P.sb("mIT", W, BF16)
    for (dst, k0, k1) in ((mSA, 2, 3), (mST, 3, 2), (mIT, 1, 0)):
        P.copy("pool", dst[:, 0:8, :], tri[:, k0:k0 + 1, :].bc([128, 8, 128]))
        P.copy("pool", dst[:, 8:16, :], tri[:, k1:k1 + 1, :].bc([128, 8, 128]))
    grow = P.sb("grow", W, F32)
    brow = P.sb("brow", W, F32)
    eD = P.sb("eD", W, F32)
    eDT = P.sb("eDT", W, F32)
    E = eDT
    A = P.sb("A", W, F32)
    AT = P.sb("AT", W, F32)
    IT = P.sb("IT", W, F32)
    kT = P.sb("kT", [128, 4, 128], F32)
    qT = P.sb("qT", [128, 4, 128], F32)
    vT = P.sb("vT", [128, 4, 128], F32)
    Gs = P.sb("Gs", [128, 8, 128], F32)
    kTz = P.sb("kTz", [128, 4, 2, 128], F32)
    P.memset("dve", kTz.all().rearrange("p a b c -> p (a b c)"), 0.0)
    ktok = P.sb("ktok", [128, 8, 64], F32)
    vtok = P.sb("vtok", [128, 8, 64], F32)
    cols = P.sb("cols", [128, 4, 16], F32)
    kbg = P.sb("kbg", [128, 16, 64], F32)
    vb = P.sb("vbb", [128, 16, 64], F32)
    kd = P.sb("kd", [128, 16, 64], F32)
    U = kd
    WT = P.sb("WT", [128, 8, 128], F32)
    QD = Gs
    Xs = [P.sb(f"Xl{i}", [128, 8, 128], F32) for i in range(2)]
    Ys = [P.sb(f"Yl{i}", [128, 8, 128], F32) for i in range(6)]
    TT = P.sb("TT", W, F32)
    for n in chunks:
        tsl = slice(n * 128, (n + 1) * 128)
        P.dma("sp", grow.all(), V(G.GC.ap[:, tsl].partition_broadcast(128), G.GC.bufs))
        P.dma("sp", brow.all(), V(G.BETA.ap[:, tsl].partition_broadcast(128), G.BETA.bufs))
        P.dma("sp", qT.all(), V(G.cT.ap[0:512, tsl].rearrange("(c p) t -> p c t", p=128), G.cT.bufs[0:4]))
        P.dma("sp", kT.all(), V(G.cT.ap[512:1024, tsl].rearrange("(c p) t -> p c t", p=128), G.cT.bufs[4:8]))
        P.dma("sp", vT.all(), V(G.cT.ap[1024:1536, tsl].rearrange("(c p) t -> p c t", p=128), G.cT.bufs[8:12]))
        gcc = S["gccol"][:, n, :]
        btc = S["betacol"][:, n, :]
        P.tt("dve", E.all(), grow.all(), gcc[:, :, None].bc(W), ALU.subtract)
        P.ts("dve", eD.all(), E.all(), -1.0, ALU.mult, 0.0, ALU.min)
        for d in range(2):
            P.act(eD[:, d * 8:(d + 1) * 8, :], eD[:, d * 8:(d + 1) * 8, :], AF.Exp)
        P.ts("dve", eDT.all(), E.all(), 0.0, ALU.min)
        for d in range(2):
            P.act(eDT[:, d * 8:(d + 1) * 8, :], eDT[:, d * 8:(d + 1) * 8, :], AF.Exp)
        P.act(cols[:, 0, :], gcc, AF.Exp)
        P.tt("dve", cols[:, 1, :], cols[:, 0, :], btc, ALU.mult)
        P.tt("dve", cols[:, 2, :], S["GLb"][:, :, n], gcc, ALU.subtract)
        P.act(cols[:, 2, :], cols[:, 2, :], AF.Exp)
        if CUT <= 1:
            continue
        pk, pv = P.psum[0], P.psum[1]
        for c in range(4):
            P.transpose(pk[:, c * 128:(c + 1) * 128], kT[:, c, :], G.ident.all())
            P.transpose(pv[:, c * 128:(c + 1) * 128], vT[:, c, :], G.ident.all())
        P.copy("act", ktok.all(), pk.all().rearrange("p (h d) -> p h d", d=64))
        P.copy("act", vtok.all(), pv.all().rearrange("p (h d) -> p h d", d=64))
        for d in range(2):
            cs = slice(d * 8, (d + 1) * 8)
            P.tt("dve", kbg[:, cs, :], ktok.all(), cols[:, 1, cs][:, :, None].bc([128, 8, 64]), ALU.mult)
            P.tt("dve", vb[:, cs, :], vtok.all(), btc[:, cs][:, :, None].bc([128, 8, 64]), ALU.mult)
            P.tt("dve", kd[:, cs, :], ktok.all(), cols[:, 2, cs][:, :, None].bc([128, 8, 64]), ALU.mult)
        P.dma("sp", G.dKD.at(n, (n,)), kd.all())
        if CUT <= 2:
            continue
        pg = [P.psum[2], P.psum[3]]
        pq = [P.psum[4], P.psum[5]]
        P.copy("dve", kTz[0:64, :, 0, :], kT[0:64])
        P.copy("dve", kTz[64:128, :, 1, :], kT[64:128])
        for h in range(8):
            c, hf = h // 2, h % 2
            P.matmul(pg[h // 4][:, (h % 4) * 128:(h % 4 + 1) * 128], kTz[:, c, hf, :], kT[:, c, :])
            P.matmul(pq[h // 4][:, (h % 4) * 128:(h % 4 + 1) * 128], kTz[:, c, hf, :], qT[:, c, :])
        for hh in range(2):
            P.copy("act", Gs[:, hh * 4:(hh + 1) * 4, :], pg[hh].all().rearrange("p (h f) -> p h f", f=128))
        if CUT <= 3:
            continue
        P.tt("dve", eD.all(), eD.all(), mSA.all(), ALU.mult)
        P.tt("pool", IT.all(), eDT.all(), mIT.all(), ALU.mult)
        P.tt("pool", eDT.all(), eDT.all(), mST.all(), ALU.mult)
        for d in range(2):
            cs = slice(d * 8, (d + 1) * 8)
            P.tt("dve", A[:, cs, :], eD[:, cs, :], Gs.all(), ALU.mult)
            P.tt("pool", AT[:, cs, :], eDT[:, cs, :], Gs.all(), ALU.mult)
            for hh in range(2):
                P.tt("dve", IT[:, d * 8 + hh * 4:d * 8 + hh * 4 + 4, :], IT[:, d * 8 + hh * 4:d * 8 + hh * 4 + 4, :],
                     pq[hh].all().rearrange("p (h f) -> p h f", f=128), ALU.mult)
        P.tt("dve", A.all(), A.all(), btc[:, :, None].bc(W), ALU.mult)
        P.tt("pool", AT.all(), AT.all(), brow.all(), ALU.mult)
        P.dma("sp", G.dIT.at(n, (n,)), IT.all())
        if CUT <= 4:
            continue
        for d in range(2):
            P.act(grow[:, d * 8:(d + 1) * 8, :], grow[:, d * 8:(d + 1) * 8, :], AF.Exp)
        g5 = grow.all().rearrange("p (d hp par) t -> p d hp par t", d=2, hp=4, par=2)
        qd4 = QD.all().rearrange("p (d hp) t -> p d hp t", d=2)
        for hf in range(2):
            psl = slice(hf * 64, (hf + 1) * 64)
            for d in range(2):
                P.tt("dve", qd4[psl, d], qT[psl], g5[psl, d, :, hf, :], ALU.mult)
        P.dma("sp", G.dQD.at(n, (n,)), QD.all())
        if CUT <= 5:
            continue
        for d in range(2):
            cs = slice(d * 8, (d + 1) * 8)
            Xc, Yc = AT[:, cs, :], A[:, cs, :]
            P.ts("dve", TT[:, cs, :], Xc, -1.0, ALU.mult)
            P.tt("dve", TT[:, cs, :], TT[:, cs, :], G.ident[:, None, :].bc([128, 8, 128]), ALU.add)
            for k in range(6):
                px = [P.psum[0], P.psum[1]]
                py = [P.psum[2], P.psum[3]]
                Xn = Xs[k % 2]
                Yn = Ys[k]
                for c in range(8):
                    osl = slice((c % 4) * 128, (c % 4 + 1) * 128)
                    if k < 5:
                        P.matmul(px[c // 4][:, osl], Yc[:, c, :], Xc[:, c, :])
                    P.matmul(py[c // 4][:, osl], Xc[:, c, :], Yc[:, c, :])
                for hh in range(2):
                    if k < 5:
                        P.copy("act", Xn[:, hh * 4:(hh + 1) * 4, :], px[hh].all().rearrange("p (h f) -> p h f", f=128))
                    P.copy("dve", Yn[:, hh * 4:(hh + 1) * 4, :], py[hh].all().rearrange("p (h f) -> p h f", f=128))
                Xc, Yc = Xn.all(), Yn.all()
            for k in range(6):
                pt = [P.psum[4], P.psum[5]]
                for c in range(8):
                    osl = slice((c % 4) * 128, (c % 4 + 1) * 128)
                    P.matmul(pt[c // 4][:, osl], Ys[k][:, c, :], TT[:, d * 8 + c, :])
                for hh in range(2):
                    P.tt("dve", TT[:, d * 8 + hh * 4:d * 8 + hh * 4 + 4, :], TT[:, d * 8 + hh * 4:d * 8 + hh * 4 + 4, :],
                         pt[hh].all().rearrange("p (h f) -> p h f", f=128), ALU.add)
        if CUT <= 6:
            continue
        pu = [P.psum[6], P.psum[7]]
        for c in range(16):
            P.matmul(pu[c // 8][:, (c % 8) * 64:(c % 8 + 1) * 64], TT[:, c, :], vb[:, c, :])
        for hh in range(2):
            P.copy("act", U[:, hh * 8:(hh + 1) * 8, :], pu[hh].all().rearrange("p (c e) -> p c e", e=64))
        P.dma("sp", G.dU.at(n, (n,)), U.all())
        if CUT <= 7:
            continue
        for d in range(2):
            pw = [P.psum[0], P.psum[1]]
            for hp in range(4):
                c0 = d * 8 + 2 * hp
                P.matmul(pw[hp // 2][:, (hp % 2) * 256:(hp % 2 + 1) * 256],
                         kbg[:, c0:c0 + 2, :].rearrange("p a b -> p (a b)"), TT[:, c0:c0 + 2, :].rearrange("p a b -> p (a b)"))
            for hh in range(2):
                v4 = pw[hh].all().rearrange("p (hp par t) -> p hp par t", hp=2, par=2)
                P.copy("act", WT[0:64, d * 4 + hh * 2:d * 4 + hh * 2 + 2, :], v4[0:64, :, 0, :])
                P.copy("dve", WT[64:128, d * 4 + hh * 2:d * 4 + hh * 2 + 2, :], v4[64:128, :, 1, :])
        P.dma("sp", G.dWT.at(n, (n,)), WT.all())
    P.release(m)


def stage_gdn_scan(G, S, need_ctx=True):
    P = G.P
    m = P.mark()
    order0 = [32, 33] + list(range(32))
    order1 = [33, 32] + list(range(31, -1, -1))
    Sall = P.sb("Sall", [128, 8, 128], F32)
    P.memset("dve", Sall.all().rearrange("p a b -> p (a b)"), 0.0)
    blk = P.sb("sblk", [128, 128], F32)
    P.dma("sp", blk.all(), G.c_blkones.all())
    SD5 = S["SD"].all().rearrange("p (d hp par) n -> p d hp par n", d=2, hp=4, par=2)
    bufs = []
    for i in range(2):
        bufs.append(dict(U=P.sb(f"sU{i}", [128, 16, 64], F32), KD=P.sb(f"sKD{i}", [128, 16, 64], F32),
                         WT=P.sb(f"sWT{i}", [128, 8, 128], F32), QD=P.sb(f"sQD{i}", [128, 8, 128], F32),
                         IT=P.sb(f"sIT{i}", [128, 16, 128], F32), vn=P.sb(f"svn{i}", [128, 16, 64], F32),
                         o=P.sb(f"so{i}", [128, 16, 64], F32), sd=P.sb(f"ssd{i}", [128, 8], F32),
                         tmp=P.sb(f"stmp{i}", [128, 8, 128], F32)))
    for s in range(NCH):
        B = bufs[s % 2]
        ns = (order0[s], order1[s])
        for d in range(2):
            n = ns[d]
            P.dma("sp", B["U"][:, d * 8:(d + 1) * 8, :], G.dU.at(n, (n, slice(None), slice(d * 8, (d + 1) * 8))))
            P.dma("sp", B["KD"][:, d * 8:(d + 1) * 8, :], G.dKD.at(n, (n, slice(None), slice(d * 8, (d + 1) * 8))))
            P.dma("sp", B["IT"][:, d * 8:(d + 1) * 8, :], G.dIT.at(n, (n, slice(None), slice(d * 8, (d + 1) * 8))))
            P.dma("sp", B["WT"][:, d * 4:(d + 1) * 4, :], G.dWT.at(n, (n, slice(None), slice(d * 4, (d + 1) * 4))))
            P.dma("sp", B["QD"][:, d * 4:(d + 1) * 4, :], G.dQD.at(n, (n, slice(None), slice(d * 4, (d + 1) * 4))))
            for hf in range(2):
                psl = slice(hf * 64, (hf + 1) * 64)
                P.copy("dve", B["sd"][psl, d * 4:(d + 1) * 4], SD5[psl, d, :, hf, n])
        p1 = [P.psum[0], P.psum[1]]
        for j in range(8):
            P.matmul(p1[j // 4][:, (j % 4) * 128:(j % 4 + 1) * 128], B["WT"][:, j, :], Sall[:, j, :])
        for hh in range(2):
            P.tt("dve", B["vn"][:, hh * 8:(hh + 1) * 8, :], B["U"][:, hh * 8:(hh + 1) * 8, :],
                 p1[hh].all().rearrange("p (c e) -> p c e", e=64), ALU.subtract)
        p2 = [P.psum[2], P.psum[3]]
        for j in range(8):
            osl = slice((j % 4) * 128, (j % 4 + 1) * 128)
            P.matmul(p2[j // 4][:, osl], B["QD"][:, j, :], Sall[:, j, :], start=True, stop=False)
            for par in range(2):
                c = 2 * j + par
                o2 = slice((j % 4) * 128 + par * 64, (j % 4) * 128 + (par + 1) * 64)
                P.matmul(p2[j // 4][:, o2], B["IT"][:, c, :], B["vn"][:, c, :], start=False, stop=(par == 1))
        for d in range(2):
            n = ns[d]
            P.copy("act", B["o"][:, d * 8:(d + 1) * 8, :], p2[d].all().rearrange("p (c e) -> p c e", e=64))
            if n < 32 or need_ctx:
                P.dma("sp", G.O2.at(d, (d, slice(n * 128, (n + 1) * 128), slice(None))),
                      B["o"][:, d * 8:(d + 1) * 8, :].rearrange("p c e -> p (c e)"))
        p3 = [P.psum[4], P.psum[5]]
        for j in range(8):
            c0 = 2 * j
            P.matmul(p3[j // 4][:, (j % 4) * 128:(j % 4 + 1) * 128], B["KD"][:, c0:c0 + 2, :].rearrange("p a b -> p (a b)"),
                     B["vn"][:, c0:c0 + 2, :].rearrange("p a b -> p (a b)"))
        for hh in range(2):
            P.tt("dve", B["tmp"][:, hh * 4:(hh + 1) * 4, :], p3[hh].all().rearrange("p (j f) -> p j f", f=128),
                 blk[:, None, :].bc([128, 4, 128]), ALU.mult)
        P.tt("dve", Sall.all(), Sall.all(), B["sd"][:, :, None].bc([128, 8, 128]), ALU.mult)
        P.tt("dve", Sall.all(), Sall.all(), B["tmp"].all(), ALU.add)
    P.release(m)


def stage_gdn_final(G, l, need_ctx=True):
    P = G.P
    m = P.mark()
    og = P.sb("og", [128, 64], F32)
    P.dma("sp", og.all(), V(G.o_gain.ap[l].partition_broadcast(128), G.o_gain.bufs))
    bufs = [dict(a=P.sb(f"fa{i}", [128, 8, 64], F32), b=P.sb(f"fb{i}", [128, 8, 64], F32), sq=P.sb(f"fsq{i}", [128, 8, 64], F32),
                 ss=P.sb(f"fss{i}", [128, 8], F32), sg=P.sb(f"fsg{i}", [128, 512], BF16), ob=P.sb(f"fob{i}", [128, 4, 128], BF16))
            for i in range(2)]
    for t in range(NTILE):
        if t >= 32 and not need_ctx:
            continue
        B = bufs[t % 2]
        rows = slice(t * 128, (t + 1) * 128)
        P.dma("sp", B["a"].all().rearrange("p h e -> p (h e)"), G.O2.at(0, (0, rows, slice(None))))
        P.dma("sp", B["b"].all().rearrange("p h e -> p (h e)"), G.O2.at(1, (1, rows, slice(None))))
        P.dma("sp", B["sg"].all(), G.sgC[rows])
        P.tt("dve", B["a"].all(), B["a"].all(), B["b"].all(), ALU.add)
        P.tt("pool", B["sq"].all(), B["a"].all(), B["a"].all(), ALU.mult)
        P.reduce("dve", B["ss"].all(), B["sq"].all(), ALU.add)
        P.act(B["ss"].all(), B["ss"].all(), AF.Sqrt, bias=G.eps.all(), scale=1.0 / 64)
        P.recip(B["ss"].all(), B["ss"].all())
        P.tt("dve", B["a"].all(), B["a"].all(), B["ss"][:, :, None].bc([128, 8, 64]), ALU.mult)
        P.tt("pool", B["a"].all(), B["a"].all(), og[:, None, :].bc([128, 8, 64]), ALU.mult)
        P.tt("dve", B["a"].all().rearrange("p h e -> p (h e)"), B["a"].all().rearrange("p h e -> p (h e)"), B["sg"].all(), ALU.mult)
        ps = P.psum[t % 2]
        for c in range(4):
            P.transpose(ps[:, c * 128:(c + 1) * 128], B["a"].all().rearrange("p h e -> p (h e)")[:, c * 128:(c + 1) * 128], G.ident.all())
        P.copy("act", B["ob"].all(), ps.all().rearrange("p (c t) -> p c t", t=128))
        P.dma("sp", V(G.ocT.ap[:, rows].rearrange("(c p) t -> p c t", p=128), G.ocT.bufs), B["ob"].all())
    P.release(m)


def stage_gdn(G, l, need_ctx=True):
    P = G.P
    m = P.mark()
    S = gdn_persist(G)
    stage_gdn_conv(G, l)
    stage_gdn_gates(G, l, S)
    stage_gdn_prep(G, l, S, list(range(NCH)))
    stage_gdn_scan(G, S, need_ctx)
    stage_gdn_final(G, l, need_ctx)
    P.release(m)


U32 = mybir.dt.uint32
NE = 16
CAP_L, CAP_X = 512, 32
NSLOT = CAP_L + CAP_X


def setupF(G, ext_weights=True):
    P = G.P
    ext = lambda n, s, dt=F32: P.dram(n, s, dt, kind="ExternalInput")
    G.w_router = ext("w_router", [DEPTH, 1024, 16])
    if ext_weights:
        G.w_gate_up = G.wgu_full if hasattr(G, "wgu_full") else ext("w_gate_up", [DEPTH, NE, 1024, 4096])
        G.w_down = G.wd_full if hasattr(G, "wd_full") else ext("w_down", [DEPTH, NE, 2048, 1024])
    G.H2 = P.dram("H2", [NT, 1024], BF16)
    G.MOE = P.dram("MOE", [NT, 1024], F32)
    G.GV = P.dram("GV", [NE, NSLOT], F32)
    G.GI = P.dram("GI", [NE, NSLOT], U32)


def stage_ffn_route(G, l, need_ctx=True):
    P = G.P
    m = P.mark()
    affT = P.sb("affT", [16, NT], F32)
    wr = P.sb("wr", [128, 8, 16], F32)
    P.dma("sp", wr.all(), V(G.w_router.ap[l].rearrange("(k p) e -> p k e", p=128), G.w_router.bufs))
    zero = P.sb("zero", [128, 1024], F32)
    P.memset("pool", zero.all(), 0.0)
    m2 = P.mark()
    bufs = [dict(x=P.sb(f"rx{i}", [128, 1024], F32), h=P.sb(f"rh{i}", [128, 1024], F32), hb=P.sb(f"rhb{i}", [128, 1024], BF16),
                 hT=P.sb(f"rhT{i}", [128, 8, 128], F32), st=P.sb("st", [128, 2, 6], F32), mv=P.sb("mv", [128, 2], F32),
                 rstd=P.sb("rstd", [128, 1], F32), sd=P.sb("sd", [128, 1], F32), e=P.sb(f"re{i}", [128, 16], F32),
                 mx=P.sb(f"rmx{i}", [128, 1], F32), sm=P.sb(f"rsm{i}", [128, 1], F32)) for i in range(2)]
    for t in range(NTILE):
        if t >= 32 and not need_ctx:
            continue
        B = bufs[t % 2]
        w = 0 if t < 32 else 1
        rows = slice(t * 128, (t + 1) * 128)
        P.dma("pool", G.MOE[rows], zero.all())
        P.dma("sp", B["x"].all(), G.X.at(t, (rows, slice(None))))
        mv = ln_stats(P, B["x"], B)
        P.act(B["sd"].all(), mv[:, 1:2], AF.Sqrt, bias=G.eps.all(), scale=1.0)
        P.recip(B["rstd"].all(), B["sd"].all())
        P.ts("dve", B["h"].all(), B["x"].all(), mv[:, 0:1], ALU.subtract, B["rstd"].all(), ALU.mult)
        P.tt("pool", B["h"].all(), B["h"].all(), G.modbc[:, 2, w, :], ALU.mult)
        P.tt("pool", B["h"].all(), B["h"].all(), G.modbc[:, 1, w, :], ALU.add)
        P.copy("act", B["hb"].all(), B["h"].all())
        P.dma("sp", G.H2[rows], B["hb"].all())
        for hf in range(2):
            ps = P.psum[hf]
            for k4 in range(4):
                k = hf * 4 + k4
                P.transpose(ps[:, k4 * 128:(k4 + 1) * 128], B["h"][:, k * 128:(k + 1) * 128], G.ident.all())
            P.copy("act" if hf == 0 else "dve", B["hT"][:, hf * 4:(hf + 1) * 4, :], ps.all().rearrange("p (k t) -> p k t", t=128))
        pl = P.psum[2 + t % 2]
        for k in range(8):
            P.matmul(pl[:, 0:16], B["hT"][:, k, :], wr[:, k, :], start=(k == 0), stop=(k == 7))
        P.reduce("dve", B["mx"].all(), pl[:, 0:16], ALU.max)
        P.ts("dve", B["mx"].all(), B["mx"].all(), -1.0, ALU.mult)
        P.act(B["e"].all(), pl[:, 0:16], AF.Exp, bias=B["mx"].all(), scale=1.0, accum=B["sm"].all())
        P.recip(B["sm"].all(), B["sm"].all())
        P.ts("dve", B["e"].all(), B["e"].all(), B["sm"].all(), ALU.mult)
        pt = P.psum[4 + t % 2]
        P.transpose(pt[0:16, 0:128], B["e"].all(), G.ident.all())
        P.copy("act", affT[:, rows], pt[0:16, 0:128])
    P.release(m2)
    vals = P.sb("vals", [16, NSLOT], F32)
    idx = P.sb("idx", [16, NSLOT], U32)
    wk = [P.sb(f"wk{i}", [16, NT], F32) for i in range(3)]
    sets = [(0, NL, CAP_L, 0)] + ([(NL, NCX, CAP_X, CAP_L)] if need_ctx else [])
    for (c0, n, cap, s0) in sets:
        cur = affT[:, c0:c0 + n]
        if c0 > 0:
            P.memset("dve", wk[2][:, 0:NL], -1.0)
            P.copy("dve", wk[2][:, NL:NT], affT[:, NL:NT])
            cur = wk[2][:, 0:NT]
            n = NT
        for r in range(cap // 8):
            vs = vals[:, s0 + r * 8:s0 + (r + 1) * 8]
            P.generic("dve", lambda e, a=vs, b=cur: e.max(a.ap, b.ap), [cur], [vs])
            ix = idx[:, s0 + r * 8:s0 + (r + 1) * 8]
            P.generic("dve", lambda e, a=ix, b=vs, c=cur: e.max_index(a.ap, b.ap, c.ap), [vs, cur], [ix])
            if r < cap // 8 - 1:
                nx = wk[r % 2][:, 0:n]
                P.generic("dve", lambda e, a=nx, b=vs, c=cur: e.match_replace(a.ap, b.ap, c.ap, -1.0), [vs, cur], [nx])
                cur = nx
    P.dma("sp", G.GV.all(), vals.all())
    P.dma("sp", G.GI.all(), idx.all())
    P.release(m)


def stage_ffn_experts(G, l, need_ctx=True):
    P = G.P
    m = P.mark()
    ns = NSLOT if need_ctx else CAP_L
    wg = [P.sb(f"wg{i}", [128, 8, 512], BF16) for i in range(4)]
    wd = [P.sb(f"wd{i}", [128, 16, 1024], BF16) for i in range(2)]
    xs = [P.sb(f"xs{i}", [128, 1024], BF16) for i in range(2)]
    xsT = P.sb("xsT", [128, 8, NSLOT], BF16)
    aT = P.sb("aT", [128, 16, NSLOT], BF16)
    ixt = [P.sb(f"ixt{i}", [128, 5], U32) for i in range(2)]
    gvt = [P.sb(f"gvt{i}", [128, 5], F32) for i in range(2)]
    sg = [P.sb(f"sgt{i}", [128, NSLOT], F32) for i in range(2)]
    ys = [P.sb(f"ys{i}", [128, 1024], F32) for i in range(2)]
    tiles = [(s * 128, 128, False) for s in range(4)] + ([(CAP_L, CAP_X, True)] if need_ctx else [])
    nw = 0
    ny = 0
    for e in range(NE):
        ix, gv = ixt[e % 2], gvt[e % 2]
        P.dma("sp", ix[:, 0:4], V(G.GI.ap[e, 0:CAP_L].rearrange("(s p) -> p s", p=128), G.GI.bufs), allow_slow_non_contiguous=True)
        P.dma("sp", gv[:, 0:4], V(G.GV.ap[e, 0:CAP_L].rearrange("(s p) -> p s", p=128), G.GV.bufs), allow_slow_non_contiguous=True)
        if need_ctx:
            P.dma("sp", ix[0:32, 4:5], V(G.GI.ap[e, CAP_L:NSLOT].rearrange("(p o) -> p o", o=1), G.GI.bufs), allow_slow_non_contiguous=True)
            P.dma("sp", gv[0:32, 4:5], V(G.GV.ap[e, CAP_L:NSLOT].rearrange("(p o) -> p o", o=1), G.GV.bufs), allow_slow_non_contiguous=True)
        wdt = wd[e % 2]
        P.dma("pool", wdt.all(), G.wd_g.view(l, e) if hasattr(G, "wd_g") else V(G.w_down.ap[l, e].rearrange("(k p) c -> p k c", p=128), G.w_down.bufs))
        for si, (s0, n, isx) in enumerate(tiles):
            x = xs[si % 2]
            src = G.H2.ap
            P.add("pool", lambda en, x=x, n=n, src=src, ixv=ix[0:n, si:si + 1]: en.indirect_dma_start(
                out=x.ap[0:n, :], out_offset=None, in_=src, in_offset=bass.IndirectOffsetOnAxis(ap=ixv.ap, axis=0)),
                reads=G.H2.bufs + ix.bufs, writes=x.bufs, dma=True)
            for hf in range(2):
                ps = P.psum[hf]
                psb = ps.all().bitcast(BF16)
                for k4 in range(4):
                    k = hf * 4 + k4
                    P.transpose(psb[:, k4 * 128:k4 * 128 + n], x[0:n, k * 128:(k + 1) * 128], G.identb[0:n, 0:n])
                P.copy("act" if hf == 0 else "dve", xsT[:, hf * 4:(hf + 1) * 4, s0:s0 + n],
                       psb[:, 0:512].rearrange("p (k t) -> p k t", t=128)[:, :, 0:n])
        for fb in range(4):
            wgt, wut = wg[nw % 4], wg[(nw + 1) % 4]
            nw += 2
            cg, cu = slice(fb * 512, (fb + 1) * 512), slice(2048 + fb * 512, 2048 + (fb + 1) * 512)
            if hasattr(G, "wgu_g"):
                P.dma("pool", wgt.all(), G.wgu_g.view(l, e, cg))
                P.dma("pool", wut.all(), G.wgu_g.view(l, e, cu))
            else:
                P.dma("pool", wgt.all(), V(G.w_gate_up.ap[l, e, :, cg].rearrange("(k p) c -> p k c", p=128), G.w_gate_up.bufs))
                P.dma("pool", wut.all(), V(G.w_gate_up.ap[l, e, :, cu].rearrange("(k p) c -> p k c", p=128), G.w_gate_up.bufs))
            for f4 in range(4):
                f = fb * 4 + f4
                pgl, pul, px = P.psum[2 + (f % 2) * 3], P.psum[3 + (f % 2) * 3], P.psum[4 + (f % 2) * 3]
                for (wt, pl, xo) in ((wgt, pgl, 0), (wut, pul, 64)):
                    for k in range(8):
                        P.matmul(pl[:, 0:CAP_L], wt[:, k, f4 * 128:(f4 + 1) * 128], xsT[:, k, 0:CAP_L], start=(k == 0), stop=(k == 7))
                    if need_ctx:
                        for k in range(8):
                            P.matmul(px[:, xo:xo + CAP_X], wt[:, k, f4 * 128:(f4 + 1) * 128], xsT[:, k, CAP_L:NSLOT], start=(k == 0), stop=(k == 7))
                s = sg[f % 2]
                P.act(s[:, 0:CAP_L], pgl[:, 0:CAP_L], AF.Silu)
                P.tt("dve", aT[:, f, 0:CAP_L], s[:, 0:CAP_L], pul[:, 0:CAP_L], ALU.mult)
                if need_ctx:
                    P.act(s[:, CAP_L:NSLOT], px[:, 0:CAP_X], AF.Silu)
                    P.tt("dve", aT[:, f, CAP_L:NSLOT], s[:, CAP_L:NSLOT], px[:, 64:64 + CAP_X], ALU.mult)
        for si, (s0, n, isx) in enumerate(tiles):
            y = ys[ny % 2]
            ny += 1
            pp = [P.psum[0], P.psum[1]]
            for hf in range(2):
                for f in range(16):
                    P.matmul(pp[hf][0:n, :], aT[:, f, s0:s0 + n], wdt[:, f, hf * 512:(hf + 1) * 512], start=(f == 0), stop=(f == 15))
                if hf == 0:
                    P.act(y[0:n, 0:512], pp[0][0:n, :], AF.Copy, scale=gv[0:n, si:si + 1])
                else:
                    P.ts("dve", y[0:n, 512:1024], pp[1][0:n, :], gv[0:n, si:si + 1], ALU.mult)
            dst = G.MOE.ap
            P.add("pool", lambda en, y=y, n=n, dst=dst, ixv=ix[0:n, si:si + 1]: en.indirect_dma_start(
                out=dst, out_offset=bass.IndirectOffsetOnAxis(ap=ixv.ap, axis=0), in_=y.ap[0:n, :], in_offset=None,
                compute_op=ALU.add), reads=y.bufs + ix.bufs + G.MOE.bufs, writes=G.MOE.bufs, dma=True)
    P.release(m)


def stage_ffn_final(G, l, need_ctx=True):
    P = G.P
    m = P.mark()
    g_bc = P.sb("g_bc", [128, 1024], F32)
    b_bc = P.sb("b_bc", [128, 1024], F32)
    P.dma("sp", g_bc.all(), V(G.ln2_g.ap[l].partition_broadcast(128), G.ln2_g.bufs))
    P.dma("sp", b_bc.all(), V(G.ln2_b.ap[l].partition_broadcast(128), G.ln2_b.bufs))
    tms = [res_ln_tiles(P) for i in range(2)]
    mo = [P.sb(f"mo{i}", [128, 1024], F32) for i in range(2)]
    for t in range(NTILE):
        if t >= 32 and not need_ctx:
            continue
        w = 0 if t < 32 else 1
        rows = slice(t * 128, (t + 1) * 128)
        mt = mo[t % 2]
        P.dma("sp", mt.all(), G.MOE[rows])
        res_ln(G, tms[t % 2], [mt[:, 0:512], mt[:, 512:1024]], G.modbc[:, 3, w, :], g_bc, b_bc,
               G.X.at(t, (rows, slice(None))), G.X.at(t, (rows, slice(None))))
    P.release(m)


def stage_ffn(G, l, need_ctx=True):
    stage_ffn_route(G, l, need_ctx)
    stage_ffn_experts(G, l, need_ctx)
    stage_ffn_final(G, l, need_ctx)


NCORES = 8
AG = dict(
    w_mod=([DEPTH * 1024, 6144], 1),
    w_in=([DEPTH * 1024, D_IN], 1),
    w_branch=([DEPTH * 3 * 512, 1024], 1),
    w_out=([DEPTH * 1024, 1024], 1),
    w_gate_up=([DEPTH * NE * 1024, 4096], 8),
    w_down=([DEPTH * NE * 2048, 1024], 8),
)


class GatheredExperts:
    def __init__(self, pieces, rows_per_expert):
        self.pieces = pieces
        self.rpe = rows_per_expert

    def view(self, l, e, cols=None):
        g = l * NE + e
        t = self.pieces[g % 8]
        r0 = (g // 8) * self.rpe
        ap = t.ap[r0:r0 + self.rpe] if cols is None else t.ap[r0:r0 + self.rpe, cols]
        return V(ap.rearrange("(k p) c -> p k c", p=128), t.bufs)


def gather_weights(P, G):
    full = {}
    for name, (shape, npieces) in AG.items():
        R, C = shape
        rs = R // NCORES
        ext = P.dram(name + "_sh", [rs, C], F32, kind="ExternalInput")
        pr = rs // npieces
        pieces = []
        for j in range(npieces):
            shard = P.dram(f"{name}_shi{j}", [pr, C], F32)
            step = max(1, (1 << 19) // C)
            for r0 in range(0, pr, step):
                n = min(step, pr - r0)
                P.dma("pool", shard[r0:r0 + n], ext[j * pr + r0:j * pr + r0 + n])
            out = P.dram(f"{name}_full{j}", [pr * NCORES, C], F32)
            P.allgather(out.all(), shard.all())
            pieces.append(out)
        full[name] = pieces
    return full


def build_program(nlayers=DEPTH):
    nc = bass.Bass("TRN2", target_bir_lowering=False)
    P = Prog(nc)
    full = gather_weights(P, None)
    G = setup_all(P, full, nlayers)
    m = P.mark()
    cp = [P.sb(f"cpx{i}", [128, 1024], F32) for i in range(2)]
    for t in range(NTILE):
        rows = slice(t * 128, (t + 1) * 128)
        P.dma("sp", cp[t % 2].all(), G.xin[rows])
        P.dma("sp", G.X.at(t, (rows, slice(None))), cp[t % 2].all())
    P.release(m)
    for l in range(nlayers):
        need_ctx = l < DEPTH - 1
        stage_mod(G, l)
        m = P.mark()
        hT = P.sb("hT", [128, 8, NT], BF16, nbuf=NTILE)
        stage_ln_T(G, G.X, hT)
        stage_inproj(G, l, hT)
        P.release(m)
        stage_attnA(G, need_ctx)
        stage_attnB(G, l, need_ctx)
        stage_gdn(G, l, need_ctx)
        stage_merge(G, l, G.X, need_ctx)
        stage_ffn(G, l, need_ctx)
    y = P.dram("y", [NL, D], F32, kind="ExternalOutput")
    m = P.mark()
    cp = [P.sb(f"cpy{i}", [128, 1024], F32) for i in range(2)]
    for t in range(32):
        rows = slice(t * 128, (t + 1) * 128)
        P.dma("sp", cp[t % 2].all(), G.X.at(t, (rows, slice(None))))
        P.dma("sp", y[rows], cp[t % 2].all())
    P.release(m)
    P.emit()
    return nc, P


class View3:
    def __init__(self, t, rows_per_layer, sub=None):
        self.t = t
        self.bufs = t.bufs
        self.rpl = rows_per_layer
        self.sub = sub
        self.ap = self

    def __getitem__(self, k):
        if not isinstance(k, tuple):
            k = (k,)
        l = k[0]
        rest = k[1:]
        base = self.t.ap[l * self.rpl:(l + 1) * self.rpl]
        if self.sub is not None and rest:
            i = rest[0]
            rest = rest[1:]
            base = base[i * self.sub:(i + 1) * self.sub]
        if rest:
            base = base[rest if len(rest) > 1 else rest[0]]
        return base


def setup_all(P, full, nlayers):
    G = setup(P, nlayers, ext_weights=False)
    G.w_mod = View3(full["w_mod"][0], 1024)
    G.w_in = View3(full["w_in"][0], 1024)
    attn_setup(G)
    setupB(G)
    setupM(G, ext_weights=False)
    G.w_branch = View3(full["w_branch"][0], 3 * 512, sub=512)
    G.w_out = View3(full["w_out"][0], 1024)
    setupC(G)
    G.wgu_g = GatheredExperts(full["w_gate_up"], 1024)
    G.wd_g = GatheredExperts(full["w_down"], 2048)
    setupF(G, ext_weights=False)
    return G


_CACHE = {}


def kernel(**inputs):
    f32 = lambda a: np.ascontiguousarray(np.asarray(a, dtype=np.float32))
    if "prog" not in _CACHE:
        _CACHE["prog"] = build_program()
    nc, P = _CACHE["prog"]
    perm = w_in_perm()
    big = {
        "w_mod": f32(inputs["w_mod"]).reshape(AG["w_mod"][0]),
        "w_in": f32(inputs["w_in"])[:, :, perm].reshape(AG["w_in"][0]),
        "w_branch": f32(inputs["w_branch"]).reshape(AG["w_branch"][0]),
        "w_out": f32(inputs["w_out"]).reshape(AG["w_out"][0]),
        "w_gate_up": f32(inputs["w_gate_up"]).reshape(AG["w_gate_up"][0]),
        "w_down": f32(inputs["w_down"]).reshape(AG["w_down"][0]),
    }
    shared = {
        "b_mod": f32(inputs["b_mod"]), "qk_gain": f32(inputs["qk_gain"]),
        "maskB": maskB_const(), "rpbg": rpb_gather_host(f32(inputs["rpb"])),
        "ln1_g": f32(inputs["ln1_g"]), "ln1_b": f32(inputs["ln1_b"]), "ln2_g": f32(inputs["ln2_g"]), "ln2_b": f32(inputs["ln2_b"]),
        "conv_w": f32(inputs["conv_w"]), "a_log": f32(inputs["a_log"]).reshape(DEPTH, 16),
        "dt_bias": f32(inputs["dt_bias"]).reshape(DEPTH, 16), "o_gain": f32(inputs["o_gain"]),
        "w_router": f32(inputs["w_router"]),
    }
    shared.update(const_inputs())
    shared.update(gdn_consts())
    x = f32(inputs["x"])
    ctx = f32(inputs["ctx"])
    c = f32(inputs["c"])
    c_ctx = f32(inputs["c_ctx"])
    in_maps = []
    for b in range(NCORES):
        im = dict(shared)
        im["xin"] = np.concatenate([x[b], ctx[b]], 0)
        im["cvec"] = np.stack([c[b], c_ctx])
        for name, arr in big.items():
            rs = arr.shape[0] // NCORES
            im[name + "_sh"] = np.ascontiguousarray(arr[b * rs:(b + 1) * rs])
        in_maps.append(im)
    res = run_bass_kernel_spmd(nc, in_maps, core_ids=list(range(NCORES)))
    return np.stack([np.asarray(r["y"], dtype=np.float32) for r in res.results], 0)
```
